# Optimizing a Trainium2 kernel written in Bass

```python
import jax, jax.numpy as jnp
from jax import lax
import numpy as np

D_MODEL = 4096
BATCH = 4
SEQ = 2048
DEPTH = 2
DEC_BATCH = 32
DEC_SEQ = 1
PAST_LEN = 16384
PAGE_SIZE = 128

N_MIXERS = 2
N_GMLP_LAYERS = (DEPTH + N_MIXERS - 1) // N_MIXERS
N_SWA_LAYERS = DEPTH // N_MIXERS
NORM_EPS = 1e-6
CHUNK = 128
D_GMLP = D_MODEL
N_GROUPS = 16
GROUP_DIM = D_GMLP // N_GROUPS
HEAD_DIM = 64
N_HEADS = D_MODEL // HEAD_DIM
N_KV_HEADS = 8
Q_PER_KV = N_HEADS // N_KV_HEADS
WINDOW = 128
BLOCK = WINDOW
ATTN_SCALE = HEAD_DIM ** -0.5
D_FF = 256 * ((8 * D_MODEL // 3 + 255) // 256)
CONV_W = 3

kernel_name = "hybrid_chunkgmlp_swa_sink_convffn_step"


def rms_norm(x, g):
    xf = x.astype(jnp.float32)
    y = xf * lax.rsqrt(jnp.mean(xf * xf, axis=-1, keepdims=True) + NORM_EPS)
    return (y * g.astype(jnp.float32)).astype(x.dtype)


def layer_norm(x, g, b):
    xf = x.astype(jnp.float32)
    xc = xf - jnp.mean(xf, axis=-1, keepdims=True)
    y = xc * lax.rsqrt(jnp.mean(xc * xc, axis=-1, keepdims=True) + NORM_EPS)
    return (y * g.astype(jnp.float32) + b.astype(jnp.float32)).astype(x.dtype)


def chunk_gmlp(x, w_in, ln_g, ln_b, w_s, b_s, w_out):
    n, L, _ = x.shape
    z = jax.nn.gelu(x @ w_in, approximate=False)
    u, v = jnp.split(z, 2, axis=-1)
    v = layer_norm(v, ln_g, ln_b)
    pad = (-L) % CHUNK
    nc = (L + pad) // CHUNK
    vc = jnp.pad(v, ((0, 0), (0, pad), (0, 0))).reshape(n, nc, CHUNK, N_GROUPS, GROUP_DIM)
    causal = jnp.tril(jnp.ones((CHUNK, CHUNK), dtype=bool))
    w_causal = jnp.where(causal, w_s, 0.0)
    s = jnp.einsum("gts,ncsgd->nctgd", w_causal, vc) + b_s.T[:, :, None]
    s = s.reshape(n, nc * CHUNK, D_GMLP)[:, :L]
    y = (u * s) @ w_out
    start = ((L - 1) // CHUNK) * CHUNK
    return y, v[:, start:]


def qkv_project(x, w_qkv, b_qkv):
    n, L, _ = x.shape
    qkv = x @ w_qkv + b_qkv
    q, k, v = jnp.split(qkv, [N_HEADS * HEAD_DIM, (N_HEADS + N_KV_HEADS) * HEAD_DIM], axis=-1)
    return (q.reshape(n, L, N_KV_HEADS, Q_PER_KV, HEAD_DIM),
            k.reshape(n, L, N_KV_HEADS, HEAD_DIM),
            v.reshape(n, L, N_KV_HEADS, HEAD_DIM))


def sink_softmax_attention(q, k, v, mask, sinks):
    s = jnp.einsum("...qhgd,...khd->...hgqk", q, k, preferred_element_type=jnp.float32) * ATTN_SCALE
    s = jnp.where(mask, s, -jnp.inf)
    sink = sinks.astype(jnp.float32).reshape(N_KV_HEADS, Q_PER_KV, 1, 1)
    m = jnp.maximum(jnp.max(s, axis=-1, keepdims=True), sink)
    p = jnp.exp(s - m)
    p = p / (jnp.sum(p, axis=-1, keepdims=True) + jnp.exp(sink - m))
    return jnp.einsum("...hgqk,...khd->...qhgd", p.astype(v.dtype), v)


def swa_prompt(x, w_qkv, b_qkv, sinks, w_o, b_o):
    n, L, _ = x.shape
    q, k, v = qkv_project(x, w_qkv, b_qkv)
    nb = L // BLOCK
    qb = q.reshape(n, nb, BLOCK, N_KV_HEADS, Q_PER_KV, HEAD_DIM)
    kb = k.reshape(n, nb, BLOCK, N_KV_HEADS, HEAD_DIM)
    vb = v.reshape(n, nb, BLOCK, N_KV_HEADS, HEAD_DIM)

    def with_prev(t):
        prev = jnp.pad(t, ((0, 0), (1, 0), (0, 0), (0, 0), (0, 0)))[:, :-1]
        return jnp.concatenate([prev, t], axis=2)

    i = jnp.arange(BLOCK)[:, None]
    j = jnp.arange(2 * BLOCK)[None, :]
    diff = BLOCK + i - j
    kpos = (jnp.arange(nb)[:, None, None] - 1) * BLOCK + j
    mask = ((diff >= 0) & (diff <= WINDOW) & (kpos >= 0))[:, None, None]
    o = sink_softmax_attention(qb, with_prev(kb), with_prev(vb), mask, sinks)
    y = o.reshape(n, L, N_HEADS * HEAD_DIM) @ w_o + b_o
    return y, k[:, L - WINDOW:], v[:, L - WINDOW:]


def swa_sample(x, win_k, win_v, w_qkv, b_qkv, sinks, w_o, b_o):
    n, T, _ = x.shape
    q, k, v = qkv_project(x, w_qkv, b_qkv)
    kk = jnp.concatenate([win_k.astype(k.dtype), k], axis=1)
    vv = jnp.concatenate([win_v.astype(v.dtype), v], axis=1)
    i = jnp.arange(T)[:, None]
    j = jnp.arange(WINDOW + T)[None, :]
    diff = i + WINDOW - j
    mask = (diff >= 0) & (diff <= WINDOW)
    o = sink_softmax_attention(q, kk, vv, mask, sinks)
    y = o.reshape(n, T, N_HEADS * HEAD_DIM) @ w_o + b_o
    return y, kk[:, T:], vv[:, T:]


def conv_ffn(x, prev, w_up, conv_w, conv_b, w_down):
    L = x.shape[1]
    h = x @ w_up
    hp = jnp.concatenate([prev.astype(h.dtype), h], axis=1)
    c = conv_b
    for t in range(CONV_W):
        c = c + conv_w[t] * hp[:, t:t + L]
    gate, val = jnp.split(c, 2, axis=-1)
    return (jax.nn.silu(gate) * val) @ w_down, hp[:, L:]


def setup_inputs(seed: int = 0) -> dict:
    key = jax.random.key(seed)
    ks = iter(jax.random.split(key, 32))

    def nrm(shape, scale):
        return jax.random.normal(next(ks), shape, jnp.float32) * scale

    def gain(shape):
        return 1.0 + nrm(shape, 0.02)

    qkv_width = (N_HEADS + 2 * N_KV_HEADS) * HEAD_DIM
    return {
        "x_prompt": nrm((BATCH, SEQ, D_MODEL), 1.0),
        "x_sample": nrm((DEC_BATCH, DEC_SEQ, D_MODEL), 1.0),
        "cache_win_k": nrm((N_SWA_LAYERS, DEC_BATCH, WINDOW, N_KV_HEADS, HEAD_DIM), 1.0),
        "cache_win_v": nrm((N_SWA_LAYERS, DEC_BATCH, WINDOW, N_KV_HEADS, HEAD_DIM), 1.0),
        "state_conv": nrm((DEPTH, DEC_BATCH, CONV_W - 1, 2 * D_FF), 1.0),
        "norm_mix_pre": gain((DEPTH, D_MODEL)),
        "norm_mix_post": gain((DEPTH, D_MODEL)),
        "norm_ffn_pre": gain((DEPTH, D_MODEL)),
        "norm_ffn_post": gain((DEPTH, D_MODEL)),
        "gmlp_w_in": nrm((N_GMLP_LAYERS, D_MODEL, 2 * D_GMLP), D_MODEL ** -0.5),
        "gmlp_ln_g": gain((N_GMLP_LAYERS, D_GMLP)),
        "gmlp_ln_b": nrm((N_GMLP_LAYERS, D_GMLP), 0.02),
        "gmlp_w_s": nrm((N_GMLP_LAYERS, N_GROUPS, CHUNK, CHUNK), CHUNK ** -0.5),
        "gmlp_b_s": 1.0 + nrm((N_GMLP_LAYERS, N_GROUPS, CHUNK), 0.1),
        "gmlp_w_out": nrm((N_GMLP_LAYERS, D_GMLP, D_MODEL), D_GMLP ** -0.5),
        "attn_w_qkv": nrm((N_SWA_LAYERS, D_MODEL, qkv_width), D_MODEL ** -0.5),
        "attn_b_qkv": nrm((N_SWA_LAYERS, qkv_width), 0.02),
        "attn_sinks": nrm((N_SWA_LAYERS, N_HEADS), 1.0),
        "attn_w_o": nrm((N_SWA_LAYERS, N_HEADS * HEAD_DIM, D_MODEL), (N_HEADS * HEAD_DIM) ** -0.5),
        "attn_b_o": nrm((N_SWA_LAYERS, D_MODEL), 0.02),
        "ffn_w_up": nrm((DEPTH, D_MODEL, 2 * D_FF), D_MODEL ** -0.5),
        "ffn_conv_w": nrm((DEPTH, CONV_W, 2 * D_FF), CONV_W ** -0.5),
        "ffn_conv_b": nrm((DEPTH, 2 * D_FF), 0.02),
        "ffn_w_down": nrm((DEPTH, D_FF, D_MODEL), D_FF ** -0.5),
    }


def reference(x_prompt, x_sample, cache_win_k, cache_win_v, state_conv,
              norm_mix_pre, norm_mix_post, norm_ffn_pre, norm_ffn_post,
              gmlp_w_in, gmlp_ln_g, gmlp_ln_b, gmlp_w_s, gmlp_b_s, gmlp_w_out,
              attn_w_qkv, attn_b_qkv, attn_sinks, attn_w_o, attn_b_o,
              ffn_w_up, ffn_conv_w, ffn_conv_b, ffn_w_down):
    xp, xs = x_prompt, x_sample
    gv_p, gv_s, wk_p, wv_p, wk_s, wv_s, cv_p, cv_s = [], [], [], [], [], [], [], []
    for layer in range(DEPTH):
        idx = layer // N_MIXERS
        hp = rms_norm(xp, norm_mix_pre[layer])
        hs = rms_norm(xs, norm_mix_pre[layer])
        if layer % N_MIXERS == 0:
            mp, g_p = chunk_gmlp(hp, gmlp_w_in[idx], gmlp_ln_g[idx], gmlp_ln_b[idx],
                                 gmlp_w_s[idx], gmlp_b_s[idx], gmlp_w_out[idx])
            ms, g_s = chunk_gmlp(hs, gmlp_w_in[idx], gmlp_ln_g[idx], gmlp_ln_b[idx],
                                 gmlp_w_s[idx], gmlp_b_s[idx], gmlp_w_out[idx])
            gv_p.append(g_p)
            gv_s.append(g_s)
        else:
            mp, k_p, v_p = swa_prompt(hp, attn_w_qkv[idx], attn_b_qkv[idx], attn_sinks[idx],
                                      attn_w_o[idx], attn_b_o[idx])
            ms, k_s, v_s = swa_sample(hs, cache_win_k[idx], cache_win_v[idx], attn_w_qkv[idx],
                                      attn_b_qkv[idx], attn_sinks[idx], attn_w_o[idx], attn_b_o[idx])
            wk_p.append(k_p)
            wv_p.append(v_p)
            wk_s.append(k_s)
            wv_s.append(v_s)
        xp = xp + rms_norm(mp, norm_mix_post[layer])
        xs = xs + rms_norm(ms, norm_mix_post[layer])

        zero_prev = jnp.zeros((xp.shape[0], CONV_W - 1, 2 * D_FF), xp.dtype)
        fp, c_p = conv_ffn(rms_norm(xp, norm_ffn_pre[layer]), zero_prev, ffn_w_up[layer],
                           ffn_conv_w[layer], ffn_conv_b[layer], ffn_w_down[layer])
        fs, c_s = conv_ffn(rms_norm(xs, norm_ffn_pre[layer]), state_conv[layer], ffn_w_up[layer],
                           ffn_conv_w[layer], ffn_conv_b[layer], ffn_w_down[layer])
        cv_p.append(c_p)
        cv_s.append(c_s)
        xp = xp + rms_norm(fp, norm_ffn_post[layer])
        xs = xs + rms_norm(fs, norm_ffn_post[layer])
    return (xp, xs, jnp.stack(gv_p), jnp.stack(gv_s), jnp.stack(wk_p), jnp.stack(wv_p),
            jnp.stack(wk_s), jnp.stack(wv_s), jnp.stack(cv_p), jnp.stack(cv_s))
```

```python
import numpy as np
import concourse.bass as bass
import concourse.mybir as mybir
from concourse.bass_utils import run_bass_kernel_spmd

F32 = mybir.dt.float32
BF16 = mybir.dt.bfloat16
ALU = mybir.AluOpType
AF = mybir.ActivationFunctionType
AX = mybir.AxisListType

D = 4096
KC = 32
T = 384
NB = 3
S = 4
NPASS = 3
TOK = T * NPASS
DFF = 11008
FC = 86
EPS = 1e-6
NEG = -30000.0
SCALE = 0.125

ENGS = ["pe", "act", "dve", "pool", "sp"]


class Op:
    __slots__ = ("eng", "fn", "waits", "signal", "ordinal", "dma", "sigval")

    def __init__(self, eng, fn, dma):
        self.eng = eng
        self.fn = fn
        self.waits = []
        self.signal = False
        self.ordinal = 0
        self.dma = dma
        self.sigval = 0


class Prog:
    def __init__(self):
        self.ops = {e: [] for e in ENGS}
        self.res = {}
        self.waited = {e: {} for e in ENGS}
        self.dma_count = {}

    def add(self, eng, fn, reads=(), writes=(), dma=None, skip_same_dma=False):
        op = Op(eng, fn, dma)
        op.ordinal = len(self.ops[eng])
        evs = []
        for r in reads:
            st = self.res.get(r)
            if st is not None and st[0] is not None:
                evs.append(st[0])
        for w in writes:
            st = self.res.get(w)
            if st is not None:
                if st[0] is not None:
                    evs.append(st[0])
                evs.extend(st[1].values())
        if dma is not None:
            cnt = self.dma_count.get(dma, 0) + 1
            self.dma_count[dma] = cnt
            ev = ("dma", dma, cnt * 16, op)
        else:
            ev = ("eng", eng, op.ordinal, op)
        wd = self.waited[eng]
        for e in evs:
            kind, key, val, prod = e
            if skip_same_dma and kind == "dma" and key == dma:
                continue
            if kind == "eng" and key == eng:
                if eng == "pe":
                    continue
                if op.ordinal - val > 2:
                    continue
            if val <= wd.get((kind, key), -1):
                continue
            wd[(kind, key)] = val
            op.waits.append(e)
            if kind == "eng":
                prod.signal = True
        self.ops[eng].append(op)
        for r in reads:
            st = self.res.get(r)
            if st is None:
                st = [None, {}]
                self.res[r] = st
            st[1][(ev[0], ev[1])] = ev
        for w in writes:
            self.res[w] = [ev, {}]
        return op

    def emit(self, nc, block, esems, dsems):
        for e in ENGS:
            n = 0
            for op in self.ops[e]:
                if op.signal:
                    n += 1
                op.sigval = n
        engmap = {"pe": block.tensor, "act": block.scalar, "dve": block.vector,
                  "pool": block.gpsimd, "sp": block.sync}
        for e in ENGS:
            ops = self.ops[e]

            def body(engine, ops=ops, e=e):
                for op in ops:
                    for (kind, key, val, prod) in op.waits:
                        if kind == "eng":
                            engine.wait_ge(esems[key], prod.sigval)
                        else:
                            engine.wait_ge(dsems[key], val)
                    ins = op.fn(engine)
                    if op.dma is not None:
                        ins.then_inc(dsems[op.dma], 16)
                    elif op.signal:
                        ins.then_inc(esems[e], 1)

            engmap[e](body)


class Ctx:
    pass


def build_nc(cfg):
    nc = bass.Bass("TRN2", target_bir_lowering=False)
    P = Prog()
    global LASTP
    LASTP = P
    npass = cfg.get("npass", NPASS)
    stages = cfg.get("stages", ("gmlp", "ffn0", "attn", "ffn1"))
    W0 = T + S

    def din(name, shape):
        return nc.dram_tensor(name, list(shape), F32, kind="ExternalInput").ap()

    def dout(name, shape):
        return nc.dram_tensor(name, list(shape), F32, kind="ExternalOutput").ap()

    xT = din("xT", [D, TOK])
    xsT = din("xsT", [D, S])
    gains = din("gains", [128, 8 * KC])
    identf = din("identf", [128, 128])
    w_in = din("gmlp_w_in", [D, 2 * D])
    w_out = din("gmlp_w_out", [D, D])
    lngb = din("lngb", [128, 2 * KC])
    wsT = din("wsT", [128, 16 * 128])
    causT = din("causT", [128, 128])
    bsT = din("bsT", [128, 16])
    wsb0 = din("wsb0", [1, 32])
    w_up = din("ffn_w_up", [2, D, 2 * DFF])
    w_down = din("ffn_w_down", [2, DFF, D])
    convc = din("convc", [2, 128, 2 * FC * 4])
    stS = din("stS", [2, 128, 2 * FC * S * 2])
    w_qkv = din("attn_w_qkv", [D, 5120])
    w_o = din("attn_w_o", [D, D])
    abias = din("abias", [128, 80])
    bkv = din("bkv", [1, 1024])
    sinks = din("sinks", [1, 64])
    amask = din("amask", [128, 512])
    kcT = din("kcT", [S, 512, 128])
    vc = din("vc", [S, 128, 512])
    kc_nat = din("kc_nat", [S, 128, 512])

    yT = dout("yT", [D, TOK])
    ysT = dout("ysT", [D, S])
    gvT = dout("gvT", [D, 128])
    gvsT = dout("gvsT", [D, S])
    wk_o = dout("wk", [128, 512])
    wv_o = dout("wv", [128, 512])
    wks_o = dout("wks", [S, 128, 512])
    wvs_o = dout("wvs", [S, 128, 512])
    cvp_o = dout("cvp", [2, 128, 2 * FC * 2])
    cvs_o = dout("cvs", [2, 128, 2 * FC * S * 2])

    from contextlib import ExitStack
    with ExitStack() as es:
        def sb(name, shape, dt):
            return es.enter_context(nc.sbuf_tensor(name, list(shape), dt))

        XRES = sb("xres", [128, KC, W0], F32)
        R0B, R1B, R2B = KC * W0 * 2, KC * W0 * 2, 68000
        DYN = sb("dyn", [128, (R0B + R1B + R2B) // 2], BF16)
        SLOTS = [sb(f"slot{i}", [128, 4096], BF16) for i in range(3)]
        GN = sb("gn", [128, 8, KC], F32)
        IDF = sb("idf", [128, 128], F32)
        IDB = sb("idb", [128, 128], BF16)
        ONES = sb("ones", [128, 128], BF16)
        RSTD = sb("rstd", [128, W0], F32)
        TMPF = sb("tmpf", [128, W0], F32)
        SQB = [sb(f"sqb{i}", [128, W0], BF16) for i in range(2)]
        CW = sb("cw", [128, 2 * FC, 4], F32)
        CS = sb("cs", [128, 2, 2 * FC, 2], F32)
        LNGB = sb("lngbt", [128, 2, KC], F32)
        BST = sb("bst", [128, 16], F32)
        WSB0 = sb("wsb0t", [128, 32], F32)
        ABI = sb("abi", [128, 80], F32)
        KPREV = sb("kprev", [128, 8, 128], BF16)
        VPREV = sb("vprev", [128, 1024], BF16)
        SINK = sb("sink", [128, 64], F32)
        DUMMY = sb("dummyt", [128, 8], F32)
        PS = [es.enter_context(nc.psum_tensor(f"ps{i}", [128, 512], F32)) for i in range(8)]

        def carve(off_bytes, nbytes, dt):
            assert off_bytes % 4 == 0 and nbytes % 4 == 0
            assert off_bytes + nbytes <= R0B + R1B + R2B, (off_bytes, nbytes)
            v = DYN[:, off_bytes // 2:(off_bytes + nbytes) // 2]
            return v if dt == BF16 else v.bitcast(F32)

        R1 = R0B
        R2 = R0B + R1B
        XNv = carve(0, R0B, BF16).rearrange("p (k w) -> p k w", k=KC)
        Yv = carve(0, R0B + R1B, F32).rearrange("p (k w) -> p k w", k=KC)

        esems = {e: es.enter_context(nc.semaphore(f"sem_{e}")) for e in ENGS}
        dsem_names = ["slot0", "slot1", "slot2", "smpk", "smpv"] + [f"ld{i}" for i in range(6)] + [f"st{i}" for i in range(4)]
        dsems = {n: es.enter_context(nc.semaphore(f"dsem_{n}")) for n in dsem_names}

        def A(eng, fn, reads=(), writes=(), dma=None, ph=True, skip_same_dma=False):
            rd = list(reads)
            if ph:
                rd.append("PHASE")
            return P.add(eng, fn, rd, list(writes), dma, skip_same_dma)

        def barrier():
            A("dve", lambda e: e.memset(DUMMY[:], 0.0), reads=[], writes=["PHASE"], ph=False)

        slot_ctr = [0]

        def load_slot(src_ap, a, b):
            i = slot_ctr[0] % 3
            slot_ctr[0] += 1
            view = SLOTS[i][:, 0:a * b].rearrange("p (a b) -> p a b", a=a)
            res = ("slot", i)
            A("pool", lambda e, o=view, s=src_ap: e.dma_start(out=o, in_=s),
              writes=[res], dma=f"slot{i}", ph=False)
            return view, res

        rr_ctr = {"ld": 0, "st": 0}
        RR_N = {"ld": 6, "st": 4}

        def sp_dma(kind, out_ap, in_ap, reads, writes):
            i = rr_ctr[kind] % RR_N[kind]
            rr_ctr[kind] += 1
            name = f"{kind}{i}"
            op = A("sp", lambda e, o=out_ap, i_=in_ap: e.dma_start(out=o, in_=i_), reads=reads, writes=writes, dma=name)
            prev = (P.dma_count[name] - 1) * 16
            if prev > 0 and P.waited["sp"].get(("dma", name), -1) < prev:
                op.waits.append(("dma", name, prev, None))
                P.waited["sp"][("dma", name)] = prev
            return op

        def sp_load(out_ap, in_ap, writes, reads=()):
            return sp_dma("ld", out_ap, in_ap, reads, writes)

        def sp_store(out_ap, in_ap, reads, writes=()):
            return sp_dma("st", out_ap, in_ap, reads, writes)

        sp_load(GN[:].rearrange("p v k -> p (v k)"), gains, ["GN"])
        sp_load(IDF[:], identf, ["IDF"])
        sp_load(LNGB[:].rearrange("p v k -> p (v k)"), lngb, ["LNGB"])
        sp_load(BST[:], bsT, ["BST"])
        sp_load(WSB0[:], wsb0.partition_broadcast(128), ["WSB0"])
        sp_load(ABI[:], abias, ["ABI"])
        sp_load(SINK[:], sinks.partition_broadcast(128), ["SINK"])
        A("dve", lambda e: e.tensor_copy(IDB[:], IDF[:]), ["IDF"], ["IDB"])
        A("dve", lambda e: e.memset(ONES[:], 1.0), [], ["ONES"])
        A("dve", lambda e: e.memset(CS[:].rearrange("p l f r -> p (l f r)"), 0.0), [], ["CS0", "CS1"])
        A("dve", lambda e: e.memset(KPREV[:].rearrange("p h t -> p (h t)"), 0.0), [], ["KPREV"])
        A("dve", lambda e: e.memset(VPREV[:], 0.0), [], ["VPREV"])

        def stats_rstd(src_fn, src_res_fn, Wc, bank):
            for k in range(KC):
                sq = SQB[k % 2]
                A("act", lambda e, o=sq[:, :Wc], i=src_fn(k): e.activation(o, i, AF.Square),
                  src_res_fn(k), [("SQB", k % 2)])
                A("pe", lambda e, o=PS[bank][:, :Wc], r=sq[:, :Wc], k=k:
                  e.matmul(o, ONES[:], r, start=(k == 0), stop=(k == KC - 1)),
                  [("SQB", k % 2), "ONES"], [("ps", bank)])
            A("dve", lambda e: e.tensor_scalar(RSTD[:, :Wc], PS[bank][:, :Wc], 1.0 / D, EPS, ALU.mult, ALU.add),
              [("ps", bank)], ["RSTD"])
            A("act", lambda e: e.activation(RSTD[:, :Wc], RSTD[:, :Wc], AF.Sqrt), ["RSTD"], ["RSTD"])
            A("dve", lambda e: e.reciprocal(RSTD[:, :Wc], RSTD[:, :Wc]), ["RSTD"], ["RSTD"])

        pre_stats = [False]

        def finish_rstd(Wc, bank):
            A("dve", lambda e: e.tensor_scalar(RSTD[:, :Wc], PS[bank][:, :Wc], 1.0 / D, EPS, ALU.mult, ALU.add),
              [("ps", bank)], ["RSTD"])
            A("act", lambda e: e.activation(RSTD[:, :Wc], RSTD[:, :Wc], AF.Sqrt), ["RSTD"], ["RSTD"])
            A("dve", lambda e: e.reciprocal(RSTD[:, :Wc], RSTD[:, :Wc]), ["RSTD"], ["RSTD"])

        def rmsnorm_to_xn(gi, Wc):
            if pre_stats[0]:
                pre_stats[0] = False
                finish_rstd(Wc, 1)
            else:
                stats_rstd(lambda k: XRES[:, k, :Wc], lambda k: [("X", k)], Wc, 0)
            for k in range(KC):
                A("dve", lambda e, k=k: e.scalar_tensor_tensor(
                    XNv[:, k, :Wc], XRES[:, k, :Wc], GN[:, gi, k:k + 1], RSTD[:, :Wc], ALU.mult, ALU.mult),
                  [("X", k), "GN", "RSTD"], [("XN", k)])

        def post_norm_residual(gi, Wc, fuse_next=False):
            stats_rstd(lambda m: Yv[:, m, :Wc], lambda m: [("Y", m)], Wc, 0)
            for m in range(KC):
                A("dve", lambda e, m=m: e.scalar_tensor_tensor(
                    TMPF[:, :Wc], Yv[:, m, :Wc], GN[:, gi, m:m + 1], RSTD[:, :Wc], ALU.mult, ALU.mult),
                  [("Y", m), "GN", "RSTD"], ["TMPF"])
                A("dve", lambda e, m=m: e.tensor_tensor(XRES[:, m, :Wc], XRES[:, m, :Wc], TMPF[:, :Wc], ALU.add),
                  [("X", m), "TMPF"], [("X", m)])
                if fuse_next:
                    sq = SQB[m % 2]
                    A("act", lambda e, o=sq[:, :Wc], m=m: e.activation(o, XRES[:, m, :Wc], AF.Square),
                      [("X", m)], [("SQB", m % 2)])
                    A("pe", lambda e, r=sq[:, :Wc], m=m: e.matmul(PS[1][:, :Wc], ONES[:], r, start=(m == 0), stop=(m == KC - 1)),
                      [("SQB", m % 2), "ONES"], [("ps", 1)])
            if fuse_next:
                pre_stats[0] = True

        def wstat_gemm(w_ap_fn, nk, nmp, rhs_fn, rhs_res_fn, Wc, evac_fn, bank0=0):
            nkg = (nk + 15) // 16
            for mp in range(nmp):
                banks = [bank0 + 2 * (mp % 2), bank0 + 2 * (mp % 2) + 1]
                for kg in range(nkg):
                    k0 = kg * 16
                    ksz = min(16, nk - k0)
                    view, sres = load_slot(w_ap_fn(k0, ksz, mp), ksz, 256)
                    for m2 in range(2):
                        for kk in range(ksz):
                            k = k0 + kk
                            A("pe", lambda e, o=PS[banks[m2]][:, :Wc], l=view[:, kk, m2 * 128:(m2 + 1) * 128],
                              r=rhs_fn(k), k=k: e.matmul(o, l, r, start=(k == 0), stop=(k == nk - 1)),
                              [sres] + rhs_res_fn(k), [("ps", banks[m2])])
                for m2 in range(2):
                    evac_fn(2 * mp + m2, banks[m2])

        def evac_to_Y(bias_col=None):
            def f(m, bank, Wc_=None):
                pass
            return f

        def gmlp(p_i, Wc):
            ns = Wc - T
            last = (p_i == npass - 1)
            rmsnorm_to_xn(0, Wc)
            VT = carve(R2, KC * W0 * 4, F32).rearrange("p (k w) -> p k w", k=KC)
            o = R2 + KC * W0 * 4
            MEAN = carve(o, W0 * 4, F32); o += W0 * 4
            NMR = carve(o, W0 * 4, F32); o += W0 * 4
            LRS = carve(o, W0 * 4, F32); o += W0 * 4
            VB = [carve(o + i * W0 * 2, W0 * 2, BF16) for i in range(2)]; o += 2 * W0 * 2
            VQ = [carve(o + i * W0 * 2, W0 * 2, BF16) for i in range(2)]; o += 2 * W0 * 2
            VNS = carve(o, KC * S * 4, F32).rearrange("p (k s) -> p k s", k=KC); o += KC * S * 4
            VNT = carve(R1, 3 * 4096 * 2, BF16).rearrange("p (c f) -> p c f", c=3)
            w_in_v = w_in.rearrange("(k p) c -> p k c", p=128)

            def evac_v(m, bank):
                A("act", lambda e: e.activation(VT[:, m, :Wc], PS[bank][:, :Wc], AF.Gelu),
                  [("ps", bank)], [("VT", m)])
                A("dve", lambda e: e.tensor_copy(VB[m % 2][:, :Wc], VT[:, m, :Wc]), [("VT", m)], [("VB", m % 2)])
                A("act", lambda e: e.activation(VQ[m % 2][:, :Wc], VT[:, m, :Wc], AF.Square), [("VT", m)], [("VQ", m % 2)])
                A("pe", lambda e: e.matmul(PS[4][:, :Wc], ONES[:], VB[m % 2][:, :Wc], start=(m == 0), stop=(m == KC - 1)),
                  [("VB", m % 2), "ONES"], [("ps", 4)])
                A("pe", lambda e: e.matmul(PS[5][:, :Wc], ONES[:], VQ[m % 2][:, :Wc], start=(m == 0), stop=(m == KC - 1)),
                  [("VQ", m % 2), "ONES"], [("ps", 5)])

            wstat_gemm(lambda k0, ksz, mp: w_in_v[:, k0:k0 + ksz, D + mp * 256: D + (mp + 1) * 256], KC, 16,
                       lambda k: XNv[:, k, :Wc], lambda k: [("XN", k)], Wc, evac_v)
            A("dve", lambda e: e.tensor_scalar(MEAN[:, :Wc], PS[4][:, :Wc], 1.0 / D, 0.0, ALU.mult, ALU.add), [("ps", 4)], ["MEAN"])
            A("dve", lambda e: e.tensor_tensor(NMR[:, :Wc], MEAN[:, :Wc], MEAN[:, :Wc], ALU.mult), ["MEAN"], ["NMR"])
            A("dve", lambda e: e.scalar_tensor_tensor(LRS[:, :Wc], PS[5][:, :Wc], 1.0 / D, NMR[:, :Wc], ALU.mult, ALU.subtract),
              [("ps", 5), "NMR"], ["LRS"])
            A("dve", lambda e: e.tensor_scalar(LRS[:, :Wc], LRS[:, :Wc], 1.0, EPS, ALU.mult, ALU.add), ["LRS"], ["LRS"])
            A("act", lambda e: e.activation(LRS[:, :Wc], LRS[:, :Wc], AF.Sqrt), ["LRS"], ["LRS"])
            A("dve", lambda e: e.reciprocal(LRS[:, :Wc], LRS[:, :Wc]), ["LRS"], ["LRS"])
            for m in range(KC):
                A("dve", lambda e, m=m: e.tensor_tensor(VT[:, m, :Wc], VT[:, m, :Wc], MEAN[:, :Wc], ALU.subtract),
                  [("VT", m), "MEAN"], [("VT", m)])
                A("dve", lambda e, m=m: e.tensor_tensor(VT[:, m, :Wc], VT[:, m, :Wc], LRS[:, :Wc], ALU.mult),
                  [("VT", m), "LRS"], [("VT", m)])
                A("act", lambda e, m=m: e.activation(VT[:, m, :Wc], VT[:, m, :Wc], AF.Identity,
                                                     bias=LNGB[:, 1, m:m + 1], scale=LNGB[:, 0, m:m + 1]),
                  [("VT", m), "LNGB"], [("VT", m)])
            if last:
                sp_store(gvT.rearrange("(k p) t -> p k t", p=128), VT[:, :, T - 128:T], [("VT", m) for m in range(KC)])
            if ns:
                sp_store(gvsT.rearrange("(k p) t -> p k t", p=128), VT[:, :, T:T + S], [("VT", m) for m in range(KC)])
                A("dve", lambda e: e.tensor_copy(VNS[:], VT[:, :, T:T + S]), [("VT", m) for m in range(KC)], ["VNS"])
            tb = 0
            for cb in range(NB):
                for mg in range(KC // 4):
                    bank = 6 + (tb % 2)
                    tb += 1
                    for j in range(4):
                        m = mg * 4 + j
                        A("pe", lambda e, bank=bank, j=j, m=m, cb=cb: e.transpose(
                            PS[bank][:, j * 128:(j + 1) * 128], VT[:, m, cb * 128:(cb + 1) * 128], IDF[:]),
                          [("VT", m), "IDF"], [("ps", bank)])
                    A("act" if tb % 2 else "dve", (lambda e, bank=bank, mg=mg, cb=cb: e.activation(
                        VNT[:, cb, mg * 512:(mg + 1) * 512], PS[bank][:], AF.Copy)) if tb % 2 else
                      (lambda e, bank=bank, mg=mg, cb=cb: e.tensor_copy(VNT[:, cb, mg * 512:(mg + 1) * 512], PS[bank][:])),
                      [("ps", bank)], [("VNT", cb, mg)])
            barrier()
            UST = carve(R2, KC * W0 * 2, BF16).rearrange("p (k w) -> p k w", k=KC)
            o = R2 + KC * W0 * 2
            WSF = carve(o, 16 * 128 * 4, F32).rearrange("p (g t) -> p g t", g=16); o += 16 * 128 * 4
            WSB = carve(o, 16 * 128 * 2, BF16).rearrange("p (g t) -> p g t", g=16); o += 16 * 128 * 2
            CAU = carve(o, 128 * 4, F32); o += 128 * 4
            UTMP = [carve(o + i * 2048, 2048, F32) for i in range(2)]; o += 4096
            US16 = [carve(o + i * 1024, 1024, BF16) for i in range(6)]; o += 6144
            SS = carve(o, 4 * S * 4, F32).rearrange("p (j s) -> p j s", j=4); o += 4 * S * 4
            SU = carve(o, 4 * S * 4, F32).rearrange("p (j s) -> p j s", j=4); o += 4 * S * 4
            sp_load(WSF[:].rearrange("p g t -> p (g t)"), wsT, ["WSF"])
            sp_load(CAU, causT, ["CAU"])
            A("dve", lambda e: e.tensor_tensor(WSB[:], WSF[:], CAU.unsqueeze(1).to_broadcast([128, 16, 128]), ALU.mult),
              ["WSF", "CAU"], ["WSB"])
            w_in_u = w_in.rearrange("(k p) c -> p k c", p=128)
            pend = None
            for ct in range(8):
                for kq in range(4):
                    view, sres = load_slot(w_in_u[:, kq * 8:(kq + 1) * 8, ct * 512:(ct + 1) * 512], 8, 512)
                    for kk in range(8):
                        k = kq * 8 + kk
                        for cb in range(NB):
                            A("pe", lambda e, cb=cb, k=k, v=view[:, kk, :]: e.matmul(
                                PS[cb][:], XNv[:, k, cb * 128:(cb + 1) * 128], v, start=(k == 0), stop=(k == KC - 1)),
                              [sres, ("XN", k)], [("ps", cb)])
                        if ns:
                            for j in range(4):
                                A("pe", lambda e, j=j, k=k, v=view[:, kk, j * 128:(j + 1) * 128]: e.matmul(
                                    PS[3][:, j * S:(j + 1) * S], v, XNv[:, k, T:T + S], start=(k == 0 and j == 0), stop=(k == KC - 1),
                                    skip_group_check=True),
                                  [sres, ("XN", k)], [("ps", 3)])
                if pend is not None:
                    pend()
                    pend = None
                cur = []
                for cb in range(NB):
                    sbk = 4 + (cb % 2)
                    for gg in range(2):
                        g = 2 * ct + gg
                        A("pe", lambda e, sbk=sbk, gg=gg, g=g, cb=cb: e.matmul(
                            PS[sbk][:, gg * 256:(gg + 1) * 256], WSB[:, g, :], VNT[:, cb, g * 256:(g + 1) * 256],
                            start=True, stop=True),
                          ["WSB", ("VNT", cb, g // 2)], [("ps", sbk)])
                    ut = UTMP[cb % 2]
                    A("act", lambda e, ut=ut, cb=cb: e.activation(ut, PS[cb][:], AF.Gelu), [("ps", cb)], [("UTMP", cb % 2)])
                    us = US16[(ct % 2) * 3 + cb]
                    for gg in range(2):
                        g = 2 * ct + gg
                        A("dve", lambda e, us=us, ut=ut, sbk=sbk, gg=gg, g=g: e.scalar_tensor_tensor(
                            us[:, gg * 256:(gg + 1) * 256], PS[sbk][:, gg * 256:(gg + 1) * 256], BST[:, g:g + 1],
                            ut[:, gg * 256:(gg + 1) * 256], ALU.add, ALU.mult),
                          [("ps", sbk), ("UTMP", cb % 2), "BST"], [("US16", (ct % 2) * 3 + cb)])
                    cur.append(cb)
                if ns:
                    A("act", lambda e: e.activation(SU[:], PS[3][:, 0:4 * S].rearrange("p (j s) -> p j s", j=4), AF.Gelu),
                      [("ps", 3)], ["SU"])
                    for gg in range(2):
                        g = 2 * ct + gg
                        A("dve", lambda e, gg=gg, g=g, ct=ct: e.tensor_scalar(
                            SS[:, 2 * gg:2 * gg + 2, :], VNS[:, 4 * ct + 2 * gg:4 * ct + 2 * gg + 2, :],
                            WSB0[:, g:g + 1], WSB0[:, 16 + g:17 + g], ALU.mult, ALU.add),
                          ["VNS", "WSB0"], ["SS"])
                    A("dve", lambda e, ct=ct: e.tensor_tensor(UST[:, 4 * ct:4 * ct + 4, T:T + S], SS[:], SU[:], ALU.mult),
                      ["SS", "SU"], [("UST", 4 * ct + j) for j in range(4)])
                def mk_tr(ct=ct):
                    for cb in range(NB):
                        bank = 6 + (cb % 2)
                        pb = PS[bank][:].bitcast(BF16)
                        ui = (ct % 2) * 3 + cb
                        for j in range(4):
                            A("pe", lambda e, pb=pb, j=j, ui=ui: e.transpose(
                                pb[:, j * 128:(j + 1) * 128], US16[ui][:, j * 128:(j + 1) * 128], IDB[:]),
                              [("US16", ui), "IDB"], [("ps", bank)])
                        A("act", lambda e, pb=pb, cb=cb, ct=ct: e.activation(
                            UST[:, 4 * ct:4 * ct + 4, cb * 128:(cb + 1) * 128],
                            pb[:, 0:512].rearrange("p (j t) -> p j t", j=4), AF.Copy),
                          [("ps", bank)], [("UST", 4 * ct + j) for j in range(4)])
                pend = mk_tr
            if pend is not None:
                pend()
                pend = None
            barrier()
            w_out_v = w_out.rearrange("(k p) c -> p k c", p=128)

            def evac_y(m, bank):
                A("act", lambda e: e.activation(Yv[:, m, :Wc], PS[bank][:, :Wc], AF.Copy), [("ps", bank)], [("Y", m)])

            wstat_gemm(lambda k0, ksz, mp: w_out_v[:, k0:k0 + ksz, mp * 256:(mp + 1) * 256], KC, 16,
                       lambda k: UST[:, k, :Wc], lambda k: [("UST", k)], Wc, evac_y, bank0=4)
            post_norm_residual(2, Wc, fuse_next=("ffn0" in stages))
            barrier()

        def ffn(l, p_i, Wc):
            ns = Wc - T
            last = (p_i == npass - 1)
            rmsnorm_to_xn(4 + l, Wc)
            Av = carve(R2, FC * W0 * 2, BF16).rearrange("p (f w) -> p f w", f=FC)
            o = R1
            HB = [carve(o + i * (W0 + 2) * 4, (W0 + 2) * 4, F32) for i in range(2)]; o += 2 * (W0 + 2) * 4
            CB = [carve(o + i * W0 * 4, W0 * 4, F32) for i in range(2)]; o += 2 * W0 * 4
            SG = carve(o, W0 * 4, F32); o += W0 * 4
            ST = carve(o, 2 * FC * S * 2 * 4, F32).rearrange("p (f s r) -> p f s r", f=2 * FC, s=S); o += 2 * FC * S * 2 * 4
            CVS = carve(o, 2 * FC * S * 2 * 4, F32).rearrange("p (f s r) -> p f s r", f=2 * FC, s=S); o += 2 * FC * S * 2 * 4
            assert o <= R2
            sp_load(CW[:].rearrange("p f c -> p (f c)"), convc[l], ["CW"])
            if ns:
                sp_load(ST[:].rearrange("p f s r -> p (f s r)"), stS[l], ["ST"])
            w_up_v = w_up[l].rearrange("(k p) c -> p k c", p=128)
            csn = f"CS{l}"
            for fp in range(FC // 2):
                par = fp % 2
                banks = {}
                for part in range(2):
                    for kg in range(2):
                        c0 = part * DFF + fp * 256
                        view, sres = load_slot(w_up_v[:, kg * 16:(kg + 1) * 16, c0:c0 + 256], 16, 256)
                        for f2 in range(2):
                            bank = par * 4 + part * 2 + f2
                            banks[(part, f2)] = bank
                            for kk in range(16):
                                k = kg * 16 + kk
                                A("pe", lambda e, bank=bank, l_=view[:, kk, f2 * 128:(f2 + 1) * 128], k=k: e.matmul(
                                    PS[bank][:, :Wc], l_, XNv[:, k, :Wc], start=(k == 0), stop=(k == KC - 1)),
                                  [sres, ("XN", k)], [("ps", bank)])
                for f2 in range(2):
                    f = 2 * fp + f2
                    for part in range(2):
                        fi = part * FC + f
                        bank = banks[(part, f2)]
                        hb = HB[part]
                        cbt = CB[part]
                        A("act", lambda e, hb=hb, bank=bank: e.activation(hb[:, 2:2 + Wc], PS[bank][:, :Wc], AF.Copy),
                          [("ps", bank)], [("HB", part)])
                        A("act", lambda e, hb=hb, fi=fi: e.activation(hb[:, 0:2], CS[:, l, fi, :], AF.Copy),
                          [csn], [("HBh", part)])
                        A("dve", lambda e, hb=hb, cbt=cbt, fi=fi: e.tensor_scalar(
                            cbt[:, :Wc], hb[:, 2:2 + Wc], CW[:, fi, 2:3], CW[:, fi, 3:4], ALU.mult, ALU.add),
                          [("HB", part), "CW"], [("CB", part)])
                        A("dve", lambda e, hb=hb, cbt=cbt, fi=fi: e.scalar_tensor_tensor(
                            cbt[:, :T], hb[:, 1:1 + T], CW[:, fi, 1:2], cbt[:, :T], ALU.mult, ALU.add),
                          [("HB", part), ("HBh", part), "CW", ("CB", part)], [("CB", part)])
                        A("dve", lambda e, hb=hb, cbt=cbt, fi=fi: e.scalar_tensor_tensor(
                            cbt[:, :T], hb[:, 0:T], CW[:, fi, 0:1], cbt[:, :T], ALU.mult, ALU.add),
                          [("HB", part), ("HBh", part), "CW", ("CB", part)], [("CB", part)])
                        if ns:
                            A("dve", lambda e, cbt=cbt, fi=fi: e.scalar_tensor_tensor(
                                cbt[:, T:T + S], ST[:, fi, :, 1], CW[:, fi, 1:2], cbt[:, T:T + S], ALU.mult, ALU.add),
                              ["ST", "CW", ("CB", part)], [("CB", part)])
                            A("dve", lambda e, cbt=cbt, fi=fi: e.scalar_tensor_tensor(
                                cbt[:, T:T + S], ST[:, fi, :, 0], CW[:, fi, 0:1], cbt[:, T:T + S], ALU.mult, ALU.add),
                              ["ST", "CW", ("CB", part)], [("CB", part)])
                            A("act", lambda e, hb=hb, fi=fi: e.activation(CVS[:, fi, :, 1], hb[:, 2 + T:2 + T + S], AF.Copy),
                              [("HB", part)], [("CVS", fi)])
                            A("act", lambda e, fi=fi: e.activation(CVS[:, fi, :, 0], ST[:, fi, :, 1], AF.Copy),
                              ["ST"], [("CVS", fi)])
                        A("act", lambda e, hb=hb, fi=fi: e.activation(CS[:, l, fi, :], hb[:, T:T + 2], AF.Copy),
                          [("HB", part), ("HBh", part)], [csn])
                    A("act", lambda e: e.activation(SG[:, :Wc], CB[0][:, :Wc], AF.Silu), [("CB", 0)], ["SG"])
                    A("dve", lambda e, f=f: e.tensor_tensor(Av[:, f, :Wc], SG[:, :Wc], CB[1][:, :Wc], ALU.mult),
                      ["SG", ("CB", 1)], [("A", f)])
            if ns:
                sp_store(cvs_o[l], CVS[:].rearrange("p f s r -> p (f s r)"), [("CVS", fi) for fi in range(2 * FC)])
            if last:
                sp_store(cvp_o[l], CS[:, l].rearrange("p f r -> p (f r)"), [csn])
            barrier()
            w_dn_v = w_down[l].rearrange("(f p) c -> p f c", p=128)

            def evac_y(m, bank):
                A("act", lambda e: e.activation(Yv[:, m, :Wc], PS[bank][:, :Wc], AF.Copy), [("ps", bank)], [("Y", m)])

            wstat_gemm(lambda k0, ksz, mp: w_dn_v[:, k0:k0 + ksz, mp * 256:(mp + 1) * 256], FC, 16,
                       lambda k: Av[:, k, :Wc], lambda k: [("A", k)], Wc, evac_y, bank0=4)
            post_norm_residual(6 + l, Wc, fuse_next=(l == 0 and "attn" in stages))
            barrier()

        c = Ctx()
        c.__dict__.update(locals())
        build_attn(c)

        for p_i in range(npass):
            Wc = W0 if (p_i == 0 and not cfg.get('nosample')) else T
            for g4 in range(4):
                ks = slice(g4 * 8, (g4 + 1) * 8)
                sp_load(XRES[:, ks, 0:T], xT.rearrange("(k p) t -> p k t", p=128)[:, ks, p_i * T:(p_i + 1) * T],
                        [("X", k) for k in range(g4 * 8, (g4 + 1) * 8)])
            if p_i == 0:
                sp_load(XRES[:, :, T:T + S], xsT.rearrange("(k p) t -> p k t", p=128), [("X", k) for k in range(KC)])
            if "gmlp" in stages:
                gmlp(p_i, Wc)
            if "ffn0" in stages:
                ffn(0, p_i, Wc)
            if "attn" in stages:
                c.attn(p_i, Wc)
            if "ffn1" in stages:
                ffn(1, p_i, Wc)
            for g4 in range(4):
                ks = slice(g4 * 8, (g4 + 1) * 8)
                sp_store(yT.rearrange("(k p) t -> p k t", p=128)[:, ks, p_i * T:(p_i + 1) * T], XRES[:, ks, 0:T],
                         [("X", k) for k in range(g4 * 8, (g4 + 1) * 8)])
            if p_i == 0:
                sp_store(ysT.rearrange("(k p) t -> p k t", p=128), XRES[:, :, T:T + S], [("X", k) for k in range(KC)])
        A("sp", lambda e: e.nop(), reads=[], writes=["PHASE"] + [("X", k) for k in range(KC)], ph=False)
        fin = P.ops["sp"][-1]
        for i in range(4):
            stv = P.dma_count.get(f"st{i}", 0) * 16
            if stv and not any(w[0] == "dma" and w[1] == f"st{i}" and w[2] >= stv for w in fin.waits):
                fin.waits.append(("dma", f"st{i}", stv, None))

        with nc.Block() as block:
            P.emit(nc, block, esems, dsems)
    return nc


def build_attn(c):
    A, carve, PS, T_, S_ = c.A, c.carve, c.PS, T, S
    R1, R2, W0 = c.R1, c.R2, c.W0
    XNv, Yv, XRES = c.XNv, c.Yv, c.XRES
    ABI, SINK, IDB, IDF, KPREV, VPREV = c.ABI, c.SINK, c.IDB, c.IDF, c.KPREV, c.VPREV
    load_slot, sp_load, sp_store, barrier = c.load_slot, c.sp_load, c.sp_store, c.barrier
    npass = c.npass
    w_qkv, w_o = c.w_qkv, c.w_o

    def attn(p_i, Wc):
        ns = Wc - T
        last = (p_i == npass - 1)
        c.rmsnorm_to_xn(1, Wc)
        QT = carve(R2, KC * W0 * 2, BF16).rearrange("p (k w) -> p k w", k=KC)
        OT = carve(R2 + KC * W0 * 2, KC * W0 * 2, BF16).rearrange("p (k w) -> p k w", k=KC)
        KW = 128 + W0
        o = R2 + 2 * KC * W0 * 2
        KD = carve(o, 8 * KW * 2, BF16).rearrange("p (h w) -> p h w", h=8); o += 8 * KW * 2
        VDs = carve(o, 64, BF16).rearrange("p (h s) -> p h s", h=8); o += 64
        VROW = carve(o, 256, BF16); o += 256
        BKVt = carve(o, 4096, F32); o += 4096
        KS4 = carve(o, 4096, F32); o += 4096
        SETS = []
        o = 0
        for i_ in range(2):
            d_ = {}
            d_["SM"] = carve(o, 4096, F32).rearrange("p (g k) -> p g k", g=4); o += 4096
            d_["E"] = carve(o, 4096, F32).rearrange("p (g k) -> p g k", g=4); o += 4096
            d_["P16"] = carve(o, 2048, BF16).rearrange("p (g k) -> p g k", g=4); o += 2048
            d_["PT16"] = carve(o, 2048, BF16).rearrange("p (g v q) -> p g v q", g=4, v=2); o += 2048
            d_["STT"] = carve(o, 96, F32).rearrange("p (a g) -> p a g", a=6); o += 96
            SETS.append(d_)
        assert o <= R1
        o = R1
        MASK = carve(o, 2048, F32); o += 2048
        VTOKD = [carve(o + i * 2048, 2048, BF16).rearrange("p (h d) -> p h d", h=8) for i in range(4)]; o += 8192
        KSTG = carve(o, 2048, F32); o += 2048
        KSX = carve(o, 8 * 132 * 2, BF16).rearrange("p (h w) -> p h w", h=8); o += 8 * 132 * 2
        VCD = carve(o, 2048, BF16).rearrange("p (h r d) -> p h r d", h=8, r=2); o += 2048
        VROWS = [carve(o + i * 256, 256, BF16) for i in range(2)]; o += 512
        assert o <= R2
        sp_load(MASK, c.amask, ["MASK"])
        sp_load(BKVt, c.bkv.partition_broadcast(128), ["BKVt"])
        A("dve", lambda e: e.tensor_copy(KD[:, :, 0:128], KPREV[:]), ["KPREV"], ["KDp"])
        wq = w_qkv.rearrange("(k p) c -> p k c", p=128)

        def evac_q(m, bank):
            A("act", lambda e: e.activation(QT[:, m, :Wc], PS[bank][:, :Wc], AF.Identity, bias=ABI[:, m:m + 1]),
              [("ps", bank), "ABI"], [("QT", m)])

        c.wstat_gemm(lambda k0, ksz, mp: wq[:, k0:k0 + ksz, mp * 256:(mp + 1) * 256], KC, 16,
                     lambda k: XNv[:, k, :Wc], lambda k: [("XN", k)], Wc, evac_q, bank0=0)
        wkv = w_qkv.rearrange("(k p) (hh d) -> p k hh d", p=128, d=64)
        for hq in range(4 if ns else 2):
            isv = hq >= 2
            for k4 in range(8):
                i = c.slot_ctr[0] % 3
                c.slot_ctr[0] += 1
                view = c.SLOTS[i][:, 0:2048].rearrange("p (k hh r d) -> p k hh r d", k=4, hh=4, r=2)
                sres = ("slot", i)
                for r_ in range(2):
                    for kk in range(4):
                        A("pool", lambda e, o_=view[:, kk, :, r_, :], s_=wkv[:, k4 * 4 + kk, 64 + hq * 4:64 + (hq + 1) * 4, :]:
                          e.dma_start(out=o_, in_=s_), [], [sres], dma=f"slot{i}", ph=False, skip_same_dma=True)
                for hh in range(4):
                    for kk in range(4):
                        k = k4 * 4 + kk
                        lw = view[:, kk, hh].rearrange("p r d -> p (r d)")
                        if isv:
                            A("pe", lambda e, hh=hh, lw=lw, k=k: e.matmul(PS[4 + hh][:, :S], lw, XNv[:, k, T:T + S],
                                                                         start=(k == 0), stop=(k == KC - 1)),
                              [sres, ("XN", k)], [("ps", 4 + hh)])
                        else:
                            A("pe", lambda e, hh=hh, lw=lw, k=k: e.matmul(PS[4 + hh][:, :Wc], lw, XNv[:, k, :Wc],
                                                                         start=(k == 0), stop=(k == KC - 1)),
                              [sres, ("XN", k)], [("ps", 4 + hh)])
            for hh in range(4):
                h = (hq % 2) * 4 + hh
                if isv:
                    A("act", lambda e, hh=hh, h=h: e.activation(VDs[:, h, :], PS[4 + hh][:, :S], AF.Identity,
                                                              bias=ABI[:, 40 + h:41 + h]),
                      [("ps", 4 + hh), "ABI"], ["VDs"])
                else:
                    A("act", lambda e, hh=hh, h=h: e.activation(KD[:, h, 128:128 + Wc], PS[4 + hh][:, :Wc], AF.Identity,
                                                              bias=ABI[:, 32 + h:33 + h]),
                      [("ps", 4 + hh), "ABI"], [("KD", h)])
        for part in ([1, 0] if (last or ns) else [1]):
            blks = list(range(NB)) if part == 1 else ([NB - 1] if last else [])
            for kq in range(4):
                view, sres = load_slot(wq[:, kq * 8:(kq + 1) * 8, D + part * 512: D + (part + 1) * 512], 8, 512)
                for kk in range(8):
                    k = kq * 8 + kk
                    for cb in blks:
                        A("pe", lambda e, cb=cb, k=k, v=view[:, kk, :]: e.matmul(
                            PS[cb][:], XNv[:, k, cb * 128:(cb + 1) * 128], v, start=(k == 0), stop=(k == KC - 1)),
                          [sres, ("XN", k)], [("ps", cb)])
                    if ns:
                        A("pe", lambda e, k=k, v=view[:, kk, :]: e.matmul(
                            PS[3][:S, :], XNv[:, k, T:T + S], v, start=(k == 0), stop=(k == KC - 1)),
                          [sres, ("XN", k)], [("ps", 3)])
            for cb in blks:
                A("dve", lambda e, cb=cb, part=part: e.tensor_tensor(KSTG, PS[cb][:], BKVt[:, part * 512:(part + 1) * 512], ALU.add),
                  [("ps", cb), "BKVt"], ["KSTG"])
                if part == 1:
                    for r_ in range(2):
                        A("act", lambda e, cb=cb, r_=r_: e.activation(
                            VTOKD[cb + 1].rearrange("p h (r d) -> p h r d", r=2)[:, :, r_, :],
                            KSTG.rearrange("p (h d) -> p h d", h=8), AF.Copy),
                          ["KSTG"], [("VTOKD", cb + 1)])
                if last and cb == NB - 1:
                    sp_store(c.wv_o if part == 1 else c.wk_o, KSTG, ["KSTG"])
            if ns:
                A("dve", lambda e, part=part: e.tensor_tensor(KS4[:S, part * 512:(part + 1) * 512], PS[3][:S, :],
                                                             BKVt[:S, part * 512:(part + 1) * 512], ALU.add),
                  [("ps", 3), "BKVt"], [("KS4", part)])
                dst = c.wvs_o if part == 1 else c.wks_o
                sp_store(dst[:, 127, :], KS4[:S, part * 512:(part + 1) * 512], [("KS4", part)])
                src = c.vc if part == 1 else c.kc_nat
                sp_store(dst[:, 0:127, :], src[:, 1:128, :], [])
        A("dve", lambda e: e.tensor_copy(VTOKD[0].rearrange("p h d -> p (h d)"), VPREV[:]), ["VPREV"], [("VTOKD", 0)])
        barrier()
        if c.cfg.get("attn_cut", 9) < 2:
            return

        ctr = [0]

        def unit(h, gh, c0, nq, kap, nk, mask, vlist, vres, kres=()):
            u = ctr[0]
            ctr[0] += 1
            st_ = SETS[u % 2]
            SM, E, P16, PT16, STT = st_["SM"], st_["E"], st_["P16"], st_["PT16"], st_["STT"]
            MX, NM, RS, DS, DEN, RINV = [STT[:, i, :] for i in range(6)]
            q_ = u % 2

            def R(n):
                return (n, q_)
            sb0 = (u % 2) * 2
            sk = SINK[:nq, h * 8 + gh * 4: h * 8 + gh * 4 + 4]
            for gi in range(4):
                g = gh * 4 + gi
                m, half = 4 * h + g // 2, g % 2
                bank, col = sb0 + half, (gi // 2) * 256
                A("pe", lambda e, bank=bank, col=col, m=m, half=half: e.matmul(
                    PS[bank][:nq, col:col + nk], QT[half * 64:(half + 1) * 64, m, c0:c0 + nq], kap(half), start=True, stop=True),
                  [("QT", m)] + list(kres), [("ps", bank)])
            for b in range(2):
                src = PS[sb0 + b][:nq, :].rearrange("p (a k) -> p a k", a=2)[:, :, :nk]
                if mask is not None:
                    A("dve", lambda e, b=b, src=src: e.tensor_tensor(SM[:nq, b:4:2, :nk], src,
                                                                   mask.unsqueeze(1).to_broadcast([nq, 2, nk]), ALU.add),
                      [("ps", sb0 + b), "MASK"], [R("SM")])
                else:
                    A("dve", lambda e, b=b, src=src: e.tensor_copy(SM[:nq, b:4:2, :nk], src),
                      [("ps", sb0 + b)], [R("SM")])
            A("dve", lambda e: e.tensor_reduce(MX[:nq, :], SM[:nq, :, :nk], AX.X, ALU.max), [R("SM")], [R("MX")])
            A("dve", lambda e: e.scalar_tensor_tensor(MX[:nq, :], MX[:nq, :], SCALE, sk, ALU.mult, ALU.max), [R("MX"), "SINK"], [R("MX")])
            A("dve", lambda e: e.tensor_scalar(NM[:nq, :], MX[:nq, :], -1.0, 0.0, ALU.mult, ALU.add), [R("MX")], [R("NM")])
            A("dve", lambda e: e.tensor_tensor(DS[:nq, :], sk, MX[:nq, :], ALU.subtract), [R("MX"), "SINK"], [R("DS")])
            yield
            for gi in range(4):
                A("act", lambda e, gi=gi: e.activation(E[:nq, gi, :nk], SM[:nq, gi, :nk], AF.Exp,
                                                     bias=NM[:nq, gi:gi + 1], scale=SCALE), [R("SM"), R("NM")], [R("E")])
            A("act", lambda e: e.activation(DS[:nq, :], DS[:nq, :], AF.Exp), [R("DS")], [R("DS")])
            yield
            A("dve", lambda e: e.tensor_reduce(RS[:nq, :], E[:nq, :, :nk], AX.X, ALU.add), [R("E")], [R("RS")])
            A("dve", lambda e: e.tensor_tensor(DEN[:nq, :], RS[:nq, :], DS[:nq, :], ALU.add), [R("RS"), R("DS")], [R("DEN")])
            A("dve", lambda e: e.reciprocal(RINV[:nq, :], DEN[:nq, :]), [R("DEN")], [R("RINV")])
            A("dve", lambda e: e.tensor_tensor(P16[:nq, :, :nk], E[:nq, :, :nk],
                                               RINV[:nq, :].unsqueeze(2).to_broadcast([nq, 4, nk]), ALU.mult),
              [R("E"), R("RINV")], [R("P16")])
            yield
            tbank = 4 + (u % 2)
            pb = PS[tbank][:].bitcast(BF16)
            for gi in range(4):
                for vi, (vl, c_, n) in enumerate(vlist):
                    if n == 1 and nq == 1:
                        continue
                    A("pe", lambda e, gi=gi, vi=vi, c_=c_, n=n: e.transpose(
                        pb[:n, (gi * 2 + vi) * 128:(gi * 2 + vi) * 128 + nq], P16[:nq, gi, c_:c_ + n], IDB[:nq, :nq]),
                      [R("P16"), "IDB"], [("ps", tbank)])
            if nq == 128:
                A("act", lambda e: e.activation(PT16[:], pb.rearrange("p (g v q) -> p g v q", g=4, v=2), AF.Copy),
                  [("ps", tbank)], [R("PT16")])
            else:
                A("act", lambda e: e.activation(PT16[:, :, 0, 0:nq],
                                                pb.rearrange("p (g v q) -> p g v q", g=4, v=2)[:, :, 0, 0:nq], AF.Copy),
                  [("ps", tbank)], [R("PT16")])
            yield
            obank = 6 + (u % 2)
            for gi in range(4):
                for vi, (vl, c_, n) in enumerate(vlist):
                    rhs = P16[0:1, gi, c_:c_ + 1] if (n == 1 and nq == 1) else PT16[:n, gi, vi, :nq]
                    A("pe", lambda e, gi=gi, vi=vi, vl=vl, rhs=rhs: e.matmul(
                        PS[obank][:, gi * 128:gi * 128 + nq], vl, rhs, start=(vi == 0), stop=(vi == len(vlist) - 1)),
                      [R("PT16"), R("P16")] + list(vres), [("ps", obank)])
            for gi in range(4):
                g = gh * 4 + gi
                m, half = 4 * h + g // 2, g % 2
                A("act", lambda e, gi=gi, m=m, half=half: e.activation(
                    OT[half * 64:(half + 1) * 64, m, c0:c0 + nq], PS[obank][half * 64:(half + 1) * 64, gi * 128:gi * 128 + nq], AF.Copy),
                  [("ps", obank)], [("OT", m)])

        def run_units(gens):
            active = []
            it = iter(gens)
            done = False
            while True:
                while len(active) < 2 and not done:
                    try:
                        active.append(next(it))
                    except StopIteration:
                        done = True
                if not active:
                    break
                for g_ in list(active):
                    try:
                        next(g_)
                    except StopIteration:
                        active.remove(g_)

        def prompt_units():
            for cb in range(NB):
                mk = MASK[:, 256:512] if (p_i == 0 and cb == 0) else MASK[:, 0:256]
                for h in range(8):
                    vl = [(VTOKD[cb][:, h, :], 0, 128), (VTOKD[cb + 1][:, h, :], 128, 128)]
                    for gh in range(2):
                        yield unit(h, gh, cb * 128, 128,
                                   lambda half, h=h, cb=cb: KD[half * 64:(half + 1) * 64, h, cb * 128:cb * 128 + 256],
                                   256, mk, vl, [])

        def all_sample_units():
            for s in range(S):
                ksrc = c.kcT[s].rearrange("(h d) t -> d h t", d=64)
                for r_ in range(2):
                    A("pool", lambda e, r_=r_, ksrc=ksrc: e.dma_start(out=KSX[r_ * 64:(r_ + 1) * 64, :, 0:128], in_=ksrc),
                      [], ["KEYS"], dma="smpk", skip_same_dma=True)
                    A("pool", lambda e, r_=r_, s=s: e.dma_start(out=VCD[:, :, r_, :],
                                                               in_=c.vc[s].rearrange("t (h d) -> t h d", h=8)),
                      [], ["VALS"], dma="smpv", skip_same_dma=True)
                A("dve", lambda e, s=s: e.tensor_copy(KSX[:, :, 128:129], KD[:, :, 128 + T + s:128 + T + s + 1]),
                  [("KD", h) for h in range(8)], ["KEYS"])

                def sample_units(s=s):
                    for h in range(8):
                        vr = VROWS[h % 2]
                        pbv = PS[5][:].bitcast(BF16) if False else None
                        yield_prep(h, s, vr)
                        vl = [(VCD[:, h].rearrange("p r d -> p (r d)"), 0, 128), (vr[0:1, :], 128, 1)]
                        for gh in range(2):
                            yield unit(h, gh, T + s, 1, lambda half, h=h: KSX[half * 64:(half + 1) * 64, h, 0:129],
                                       129, None, vl, ["VALS", ("VROW", h % 2)], ["KEYS"])

                def yield_prep(h, s, vr):
                    pbv = PS[3][:].bitcast(BF16)
                    A("pe", lambda e, h=h, s=s, pbv=pbv: e.transpose(pbv[0:1, 0:128], VDs[:, h, s:s + 1], IDB[:]),
                      ["VDs", "IDB"], [("ps", 3)])
                    A("act", lambda e, pbv=pbv, vr=vr: e.activation(vr[0:1, :], pbv[0:1, 0:128], AF.Copy),
                      [("ps", 3)], [("VROW", h % 2)])

                for un in sample_units():
                    yield un

        def interleaved():
            a, b = prompt_units(), (all_sample_units() if ns else iter(()))
            while a is not None or b is not None:
                if a is not None:
                    try:
                        yield next(a)
                    except StopIteration:
                        a = None
                if b is not None:
                    try:
                        yield next(b)
                    except StopIteration:
                        b = None

        run_units(interleaved())
        A("dve", lambda e: e.tensor_copy(KPREV[:], KD[:, :, T:T + 128]), [("KD", h) for h in range(8)], ["KPREV"])
        A("dve", lambda e: e.tensor_copy(VPREV[:], VTOKD[NB].rearrange("p h d -> p (h d)")), [("VTOKD", NB)], ["VPREV"])
        barrier()
        wo = w_o.rearrange("(k p) c -> p k c", p=128)

        def evac_y(m, bank):
            A("act", lambda e: e.activation(Yv[:, m, :Wc], PS[bank][:, :Wc], AF.Identity, bias=ABI[:, 48 + m:49 + m]),
              [("ps", bank), "ABI"], [("Y", m)])

        c.wstat_gemm(lambda k0, ksz, mp: wo[:, k0:k0 + ksz, mp * 256:(mp + 1) * 256], KC, 16,
                     lambda k: OT[:, k, :Wc], lambda k: [("OT", k)], Wc, evac_y, bank0=4)
        c.post_norm_residual(3, Wc, fuse_next=("ffn1" in c.stages))
        barrier()

    c.attn = attn


def _f32(a):
    return np.ascontiguousarray(a, dtype=np.float32)


def prep_core_inputs(inp, core, shared=None):
    s, h = core // 2, core % 2
    start = 0 if h == 0 else 2048 - TOK
    b0 = core * S
    m = {}
    m["xT"] = _f32(inp["x_prompt"][s, start:start + TOK, :].T)
    m["xsT"] = _f32(inp["x_sample"][b0:b0 + S, 0, :].T)

    def fm(v):
        return np.asarray(v).reshape(KC, 128).T

    if shared is None:
        shared = {}
    if not shared:
        g = [inp["norm_mix_pre"][0], inp["norm_mix_pre"][1], inp["norm_mix_post"][0], inp["norm_mix_post"][1],
             inp["norm_ffn_pre"][0], inp["norm_ffn_pre"][1], inp["norm_ffn_post"][0], inp["norm_ffn_post"][1]]
        shared["gains"] = _f32(np.concatenate([fm(v) for v in g], axis=1))
        shared["identf"] = np.eye(128, dtype=np.float32)
        shared["gmlp_w_in"] = _f32(inp["gmlp_w_in"][0])
        shared["gmlp_w_out"] = _f32(inp["gmlp_w_out"][0])
        shared["lngb"] = _f32(np.concatenate([fm(inp["gmlp_ln_g"][0]), fm(inp["gmlp_ln_b"][0])], axis=1))
        ws = np.asarray(inp["gmlp_w_s"][0])
        shared["wsT"] = _f32(ws.transpose(2, 0, 1).reshape(128, 16 * 128))
        shared["causT"] = _f32(np.triu(np.ones((128, 128), np.float32)))
        bs = np.asarray(inp["gmlp_b_s"][0])
        shared["bsT"] = _f32(bs.T)
        shared["wsb0"] = _f32(np.concatenate([ws[:, 0, 0], bs[:, 0]])[None, :])
        shared["ffn_w_up"] = _f32(inp["ffn_w_up"])
        shared["ffn_w_down"] = _f32(inp["ffn_w_down"])
        cw = np.asarray(inp["ffn_conv_w"])
        cb = np.asarray(inp["ffn_conv_b"])
        cc = np.concatenate([cw, cb[:, None, :]], axis=1)
        shared["convc"] = _f32(cc.reshape(2, 4, 2 * FC, 128).transpose(0, 3, 2, 1).reshape(2, 128, 2 * FC * 4))
        shared["attn_w_qkv"] = _f32(inp["attn_w_qkv"][0])
        shared["attn_w_o"] = _f32(inp["attn_w_o"][0])
        bq = np.asarray(inp["attn_b_qkv"][0])
        bkd = np.stack([np.tile(bq[4096 + hh * 64: 4096 + (hh + 1) * 64], 2) for hh in range(16)], axis=1)
        shared["abias"] = _f32(np.concatenate([fm(bq[:4096]), bkd, fm(inp["attn_b_o"][0])], axis=1))
        shared["bkv"] = _f32(bq[4096:][None, :])
        shared["sinks"] = _f32(np.asarray(inp["attn_sinks"][0])[None, :])
        i = np.arange(128)[:, None]
        j = np.arange(256)[None, :]
        ok = (j <= 128 + i) & (j >= i)
        m1 = np.where(ok, 0.0, NEG).astype(np.float32)
        m0 = np.where(ok & (j >= 128), 0.0, NEG).astype(np.float32)
        shared["amask"] = _f32(np.concatenate([m1, m0], axis=1))
    m.update(shared)
    st = np.asarray(inp["state_conv"])[:, b0:b0 + S]
    m["stS"] = _f32(st.reshape(2, S, 2, 2 * FC, 128).transpose(0, 4, 3, 1, 2).reshape(2, 128, 2 * FC * S * 2))
    kc = np.asarray(inp["cache_win_k"])[0, b0:b0 + S].reshape(S, 128, 512)
    vcache = np.asarray(inp["cache_win_v"])[0, b0:b0 + S].reshape(S, 128, 512)
    m["kcT"] = _f32(kc.transpose(0, 2, 1))
    m["vc"] = _f32(vcache)
    m["kc_nat"] = _f32(kc)
    return m


_NC_CACHE = {}


def kernel(**inputs):
    inp = {k: np.asarray(v) for k, v in inputs.items()}
    if "nc" not in _NC_CACHE:
        _NC_CACHE["nc"] = build_nc(dict(npass=NPASS))
    nc = _NC_CACHE["nc"]
    shared = {}
    in_maps = [prep_core_inputs(inp, c, shared) for c in range(8)]
    res = run_bass_kernel_spmd(nc, in_maps, core_ids=list(range(8)))
    r = res.results
    B, L = 4, 2048
    y_prompt = np.empty((B, L, D), np.float32)
    y_sample = np.empty((32, 1, D), np.float32)
    gvp = np.empty((1, B, 128, D), np.float32)
    gvs = np.empty((1, 32, 1, D), np.float32)
    wkp = np.empty((1, B, 128, 8, 64), np.float32)
    wvp = np.empty((1, B, 128, 8, 64), np.float32)
    wks = np.empty((1, 32, 128, 8, 64), np.float32)
    wvs = np.empty((1, 32, 128, 8, 64), np.float32)
    cvp = np.empty((2, B, 2, 2 * DFF), np.float32)
    cvs = np.empty((2, 32, 2, 2 * DFF), np.float32)
    for c in range(8):
        s, h = c // 2, c % 2
        o = r[c]
        if h == 0:
            y_prompt[s, 0:TOK] = o["yT"].T
        else:
            y_prompt[s, TOK:L] = o["yT"][:, 2 * TOK - L:].T
            gvp[0, s] = o["gvT"].T
            wkp[0, s] = o["wk"].reshape(128, 8, 64)
            wvp[0, s] = o["wv"].reshape(128, 8, 64)
            cvp[:, s] = o["cvp"].reshape(2, 128, 2 * FC, 2).transpose(0, 3, 2, 1).reshape(2, 2, 2 * DFF)
        b0 = c * S
        y_sample[b0:b0 + S, 0] = o["ysT"].T
        gvs[0, b0:b0 + S, 0] = o["gvsT"].T
        wks[0, b0:b0 + S] = o["wks"].reshape(S, 128, 8, 64)
        wvs[0, b0:b0 + S] = o["wvs"].reshape(S, 128, 8, 64)
        cvs[:, b0:b0 + S] = o["cvs"].reshape(2, 128, 2 * FC, S, 2).transpose(0, 3, 4, 2, 1).reshape(2, S, 2, 2 * DFF)
    return (y_prompt, y_sample, gvp, gvs, wkp, wvp, wks, wvs, cvp, cvs)
```

```python
import numpy as np
import concourse.bass as bass
import concourse.mybir as mybir
from concourse.bass_utils import run_bass_kernel_spmd

F32 = mybir.dt.float32
BF16 = mybir.dt.bfloat16
ALU = mybir.AluOpType
AF = mybir.ActivationFunctionType
AX = mybir.AxisListType

D = 4096
KC = 32
T = 384
NB = 3
S = 4
NPASS = 3
TOK = T * NPASS
DFF = 11008
FC = 86
EPS = 1e-6
NEG = -30000.0
SCALE = 0.125

ENGS = ["pe", "act", "dve", "pool", "sp"]


class Op:
    __slots__ = ("eng", "fn", "waits", "signal", "ordinal", "dma", "sigval")

    def __init__(self, eng, fn, dma):
        self.eng = eng
        self.fn = fn
        self.waits = []
        self.signal = False
        self.ordinal = 0
        self.dma = dma
        self.sigval = 0


class Prog:
    def __init__(self):
        self.ops = {e: [] for e in ENGS}
        self.res = {}
        self.waited = {e: {} for e in ENGS}
        self.dma_count = {}

    def add(self, eng, fn, reads=(), writes=(), dma=None, skip_same_dma=False):
        op = Op(eng, fn, dma)
        op.ordinal = len(self.ops[eng])
        evs = []
        for r in reads:
            st = self.res.get(r)
            if st is not None and st[0] is not None:
                evs.append(st[0])
        for w in writes:
            st = self.res.get(w)
            if st is not None:
                if st[0] is not None:
                    evs.append(st[0])
                evs.extend(st[1].values())
        if dma is not None:
            cnt = self.dma_count.get(dma, 0) + 1
            self.dma_count[dma] = cnt
            ev = ("dma", dma, cnt * 16, op)
        else:
            ev = ("eng", eng, op.ordinal, op)
        wd = self.waited[eng]
        for e in evs:
            kind, key, val, prod = e
            if skip_same_dma and kind == "dma" and key == dma:
                continue
            if kind == "eng" and key == eng:
                if eng == "pe":
                    continue
                if op.ordinal - val > 2:
                    continue
            if val <= wd.get((kind, key), -1):
                continue
            wd[(kind, key)] = val
            op.waits.append(e)
            if kind == "eng":
                prod.signal = True
        self.ops[eng].append(op)
        for r in reads:
            st = self.res.get(r)
            if st is None:
                st = [None, {}]
                self.res[r] = st
            st[1][(ev[0], ev[1])] = ev
        for w in writes:
            self.res[w] = [ev, {}]
        return op

    def emit(self, nc, block, esems, dsems):
        for e in ENGS:
            n = 0
            for op in self.ops[e]:
                if op.signal:
                    n += 1
                op.sigval = n
        engmap = {"pe": block.tensor, "act": block.scalar, "dve": block.vector,
                  "pool": block.gpsimd, "sp": block.sync}
        for e in ENGS:
            ops = self.ops[e]

            def body(engine, ops=ops, e=e):
                for op in ops:
                    for (kind, key, val, prod) in op.waits:
                        if kind == "eng":
                            engine.wait_ge(esems[key], prod.sigval)
                        else:
                            engine.wait_ge(dsems[key], val)
                    ins = op.fn(engine)
                    if op.dma is not None:
                        ins.then_inc(dsems[op.dma], 16)
                    elif op.signal:
                        ins.then_inc(esems[e], 1)

            engmap[e](body)


class Ctx:
    pass


def build_nc(cfg):
    nc = bass.Bass("TRN2", target_bir_lowering=False)
    P = Prog()
    global LASTP
    LASTP = P
    npass = cfg.get("npass", NPASS)
    stages = cfg.get("stages", ("gmlp", "ffn0", "attn", "ffn1"))
    W0 = T + S

    def din(name, shape):
        return nc.dram_tensor(name, list(shape), F32, kind="ExternalInput").ap()

    def dout(name, shape):
        return nc.dram_tensor(name, list(shape), F32, kind="ExternalOutput").ap()

    xT = din("xT", [D, TOK])
    xsT = din("xsT", [D, S])
    gains = din("gains", [128, 8 * KC])
    identf = din("identf", [128, 128])
    w_in = din("gmlp_w_in", [D, 2 * D])
    w_out = din("gmlp_w_out", [D, D])
    lngb = din("lngb", [128, 2 * KC])
    wsT = din("wsT", [128, 16 * 128])
    causT = din("causT", [128, 128])
    bsT = din("bsT", [128, 16])
    wsb0 = din("wsb0", [1, 32])
    w_up = din("ffn_w_up", [2, D, 2 * DFF])
    w_down = din("ffn_w_down", [2, DFF, D])
    convc = din("convc", [2, 128, 2 * FC * 4])
    stS = din("stS", [2, 128, 2 * FC * S * 2])
    w_qkv = din("attn_w_qkv", [D, 5120])
    w_o = din("attn_w_o", [D, D])
    abias = din("abias", [128, 80])
    bkv = din("bkv", [1, 1024])
    sinks = din("sinks", [1, 64])
    amask = din("amask", [128, 512])
    kcT = din("kcT", [S, 512, 128])
    vc = din("vc", [S, 128, 512])
    kc_nat = din("kc_nat", [S, 128, 512])

    yT = dout("yT", [D, TOK])
    ysT = dout("ysT", [D, S])
    gvT = dout("gvT", [D, 128])
    gvsT = dout("gvsT", [D, S])
    wk_o = dout("wk", [128, 512])
    wv_o = dout("wv", [128, 512])
    wks_o = dout("wks", [S, 128, 512])
    wvs_o = dout("wvs", [S, 128, 512])
    cvp_o = dout("cvp", [2, 128, 2 * FC * 2])
    cvs_o = dout("cvs", [2, 128, 2 * FC * S * 2])

    from contextlib import ExitStack
    with ExitStack() as es:
        def sb(name, shape, dt):
            return es.enter_context(nc.sbuf_tensor(name, list(shape), dt))

        XRES = sb("xres", [128, KC, W0], F32)
        R0B, R1B, R2B = KC * W0 * 2, KC * W0 * 2, 68000
        DYN = sb("dyn", [128, (R0B + R1B + R2B) // 2], BF16)
        SLOTS = [sb(f"slot{i}", [128, 4096], BF16) for i in range(3)]
        GN = sb("gn", [128, 8, KC], F32)
        IDF = sb("idf", [128, 128], F32)
        IDB = sb("idb", [128, 128], BF16)
        ONES = sb("ones", [128, 128], BF16)
        RSTD = sb("rstd", [128, W0], F32)
        TMPF = sb("tmpf", [128, W0], F32)
        SQB = [sb(f"sqb{i}", [128, W0], BF16) for i in range(2)]
        CW = sb("cw", [128, 2 * FC, 4], F32)
        CS = sb("cs", [128, 2, 2 * FC, 2], F32)
        LNGB = sb("lngbt", [128, 2, KC], F32)
        BST = sb("bst", [128, 16], F32)
        WSB0 = sb("wsb0t", [128, 32], F32)
        ABI = sb("abi", [128, 80], F32)
        KPREV = sb("kprev", [128, 8, 128], BF16)
        VPREV = sb("vprev", [128, 1024], BF16)
        SINK = sb("sink", [128, 64], F32)
        DUMMY = sb("dummyt", [128, 8], F32)
        PS = [es.enter_context(nc.psum_tensor(f"ps{i}", [128, 512], F32)) for i in range(8)]

        def carve(off_bytes, nbytes, dt):
            assert off_bytes % 4 == 0 and nbytes % 4 == 0
            assert off_bytes + nbytes <= R0B + R1B + R2B, (off_bytes, nbytes)
            v = DYN[:, off_bytes // 2:(off_bytes + nbytes) // 2]
            return v if dt == BF16 else v.bitcast(F32)

        R1 = R0B
        R2 = R0B + R1B
        XNv = carve(0, R0B, BF16).rearrange("p (k w) -> p k w", k=KC)
        Yv = carve(0, R0B + R1B, F32).rearrange("p (k w) -> p k w", k=KC)

        esems = {e: es.enter_context(nc.semaphore(f"sem_{e}")) for e in ENGS}
        dsem_names = ["slot0", "slot1", "slot2", "smpk", "smpv"] + [f"ld{i}" for i in range(6)] + [f"st{i}" for i in range(4)]
        dsems = {n: es.enter_context(nc.semaphore(f"dsem_{n}")) for n in dsem_names}

        def A(eng, fn, reads=(), writes=(), dma=None, ph=True, skip_same_dma=False):
            rd = list(reads)
            if ph:
                rd.append("PHASE")
            return P.add(eng, fn, rd, list(writes), dma, skip_same_dma)

        def barrier():
            A("dve", lambda e: e.memset(DUMMY[:], 0.0), reads=[], writes=["PHASE"], ph=False)

        slot_ctr = [0]

        def load_slot(src_ap, a, b):
            i = slot_ctr[0] % 3
            slot_ctr[0] += 1
            view = SLOTS[i][:, 0:a * b].rearrange("p (a b) -> p a b", a=a)
            res = ("slot", i)
            A("pool", lambda e, o=view, s=src_ap: e.dma_start(out=o, in_=s),
              writes=[res, ("slotdup", i, 0), ("slotdup", i, 1)], dma=f"slot{i}", ph=False)
            return view, res

        rr_ctr = {"ld": 0, "st": 0}
        RR_N = {"ld": 6, "st": 4}

        def sp_dma(kind, out_ap, in_ap, reads, writes):
            i = rr_ctr[kind] % RR_N[kind]
            rr_ctr[kind] += 1
            name = f"{kind}{i}"
            op = A("sp", lambda e, o=out_ap, i_=in_ap: e.dma_start(out=o, in_=i_), reads=reads, writes=writes, dma=name)
            prev = (P.dma_count[name] - 1) * 16
            if prev > 0 and P.waited["sp"].get(("dma", name), -1) < prev:
                op.waits.append(("dma", name, prev, None))
                P.waited["sp"][("dma", name)] = prev
            return op

        def sp_load(out_ap, in_ap, writes, reads=()):
            return sp_dma("ld", out_ap, in_ap, reads, writes)

        def sp_store(out_ap, in_ap, reads, writes=()):
            return sp_dma("st", out_ap, in_ap, reads, writes)

        sp_load(GN[:].rearrange("p v k -> p (v k)"), gains, ["GN"])
        sp_load(IDF[:], identf, ["IDF"])
        sp_load(LNGB[:].rearrange("p v k -> p (v k)"), lngb, ["LNGB"])
        sp_load(BST[:], bsT, ["BST"])
        sp_load(WSB0[:], wsb0.partition_broadcast(128), ["WSB0"])
        sp_load(ABI[:], abias, ["ABI"])
        sp_load(SINK[:], sinks.partition_broadcast(128), ["SINK"])
        A("dve", lambda e: e.tensor_copy(IDB[:], IDF[:]), ["IDF"], ["IDB"])
        A("dve", lambda e: e.memset(ONES[:], 1.0), [], ["ONES"])
        A("dve", lambda e: e.memset(CS[:].rearrange("p l f r -> p (l f r)"), 0.0), [], ["CS0", "CS1"])
        A("dve", lambda e: e.memset(KPREV[:].rearrange("p h t -> p (h t)"), 0.0), [], ["KPREV"])
        A("dve", lambda e: e.memset(VPREV[:], 0.0), [], ["VPREV"])

        def stats_rstd(src_fn, src_res_fn, Wc, bank):
            for k in range(KC):
                sq = SQB[k % 2]
                A("act", lambda e, o=sq[:, :Wc], i=src_fn(k): e.activation(o, i, AF.Square),
                  src_res_fn(k), [("SQB", k % 2)])
                A("pe", lambda e, o=PS[bank][:, :Wc], r=sq[:, :Wc], k=k:
                  e.matmul(o, ONES[:], r, start=(k == 0), stop=(k == KC - 1)),
                  [("SQB", k % 2), "ONES"], [("ps", bank)])
            A("dve", lambda e: e.tensor_scalar(RSTD[:, :Wc], PS[bank][:, :Wc], 1.0 / D, EPS, ALU.mult, ALU.add),
              [("ps", bank)], ["RSTD"])
            A("act", lambda e: e.activation(RSTD[:, :Wc], RSTD[:, :Wc], AF.Sqrt), ["RSTD"], ["RSTD"])
            A("dve", lambda e: e.reciprocal(RSTD[:, :Wc], RSTD[:, :Wc]), ["RSTD"], ["RSTD"])

        pre_stats = [False]

        def finish_rstd(Wc, bank):
            A("dve", lambda e: e.tensor_scalar(RSTD[:, :Wc], PS[bank][:, :Wc], 1.0 / D, EPS, ALU.mult, ALU.add),
              [("ps", bank)], ["RSTD"])
            A("act", lambda e: e.activation(RSTD[:, :Wc], RSTD[:, :Wc], AF.Sqrt), ["RSTD"], ["RSTD"])
            A("dve", lambda e: e.reciprocal(RSTD[:, :Wc], RSTD[:, :Wc]), ["RSTD"], ["RSTD"])

        def rmsnorm_to_xn(gi, Wc):
            if pre_stats[0]:
                pre_stats[0] = False
                finish_rstd(Wc, 1)
            else:
                stats_rstd(lambda k: XRES[:, k, :Wc], lambda k: [("X", k)], Wc, 0)
            for k in range(KC):
                A("dve", lambda e, k=k: e.scalar_tensor_tensor(
                    XNv[:, k, :Wc], XRES[:, k, :Wc], GN[:, gi, k:k + 1], RSTD[:, :Wc], ALU.mult, ALU.mult),
                  [("X", k), "GN", "RSTD"], [("XN", k)])

        def post_norm_residual(gi, Wc, fuse_next=False):
            stats_rstd(lambda m: Yv[:, m, :Wc], lambda m: [("Y", m)], Wc, 0)
            for m in range(KC):
                A("dve", lambda e, m=m: e.scalar_tensor_tensor(
                    TMPF[:, :Wc], Yv[:, m, :Wc], GN[:, gi, m:m + 1], RSTD[:, :Wc], ALU.mult, ALU.mult),
                  [("Y", m), "GN", "RSTD"], ["TMPF"])
                A("dve", lambda e, m=m: e.tensor_tensor(XRES[:, m, :Wc], XRES[:, m, :Wc], TMPF[:, :Wc], ALU.add),
                  [("X", m), "TMPF"], [("X", m)])
                if fuse_next:
                    sq = SQB[m % 2]
                    A("act", lambda e, o=sq[:, :Wc], m=m: e.activation(o, XRES[:, m, :Wc], AF.Square),
                      [("X", m)], [("SQB", m % 2)])
                    A("pe", lambda e, r=sq[:, :Wc], m=m: e.matmul(PS[1][:, :Wc], ONES[:], r, start=(m == 0), stop=(m == KC - 1)),
                      [("SQB", m % 2), "ONES"], [("ps", 1)])
            if fuse_next:
                pre_stats[0] = True

        def wstat_gemm(w_ap_fn, nk, nmp, rhs_fn, rhs_res_fn, Wc, evac_fn, bank0=0):
            nkg = (nk + 15) // 16
            for mp in range(nmp):
                banks = [bank0 + 2 * (mp % 2), bank0 + 2 * (mp % 2) + 1]
                for kg in range(nkg):
                    k0 = kg * 16
                    ksz = min(16, nk - k0)
                    view, sres = load_slot(w_ap_fn(k0, ksz, mp), ksz, 256)
                    for m2 in range(2):
                        for kk in range(ksz):
                            k = k0 + kk
                            A("pe", lambda e, o=PS[banks[m2]][:, :Wc], l=view[:, kk, m2 * 128:(m2 + 1) * 128],
                              r=rhs_fn(k), k=k: e.matmul(o, l, r, start=(k == 0), stop=(k == nk - 1)),
                              [sres] + rhs_res_fn(k), [("ps", banks[m2])])
                for m2 in range(2):
                    evac_fn(2 * mp + m2, banks[m2])

        def evac_to_Y(bias_col=None):
            def f(m, bank, Wc_=None):
                pass
            return f

        def gmlp(p_i, Wc):
            ns = Wc - T
            last = (p_i == npass - 1)
            rmsnorm_to_xn(0, Wc)
            VT = carve(R2, KC * W0 * 4, F32).rearrange("p (k w) -> p k w", k=KC)
            o = R2 + KC * W0 * 4
            MEAN = carve(o, W0 * 4, F32); o += W0 * 4
            NMR = carve(o, W0 * 4, F32); o += W0 * 4
            LRS = carve(o, W0 * 4, F32); o += W0 * 4
            VB = [carve(o + i * W0 * 2, W0 * 2, BF16) for i in range(2)]; o += 2 * W0 * 2
            VQ = [carve(o + i * W0 * 2, W0 * 2, BF16) for i in range(2)]; o += 2 * W0 * 2
            VNS = carve(o, KC * S * 4, F32).rearrange("p (k s) -> p k s", k=KC); o += KC * S * 4
            VNT = carve(R1, 3 * 4096 * 2, BF16).rearrange("p (c f) -> p c f", c=3)
            w_in_v = w_in.rearrange("(k p) c -> p k c", p=128)

            def evac_v(m, bank):
                A("act", lambda e: e.activation(VT[:, m, :Wc], PS[bank][:, :Wc], AF.Gelu),
                  [("ps", bank)], [("VT", m)])
                A("dve", lambda e: e.tensor_copy(VB[m % 2][:, :Wc], VT[:, m, :Wc]), [("VT", m)], [("VB", m % 2)])
                A("act", lambda e: e.activation(VQ[m % 2][:, :Wc], VT[:, m, :Wc], AF.Square), [("VT", m)], [("VQ", m % 2)])
                A("pe", lambda e: e.matmul(PS[4][:, :Wc], ONES[:], VB[m % 2][:, :Wc], start=(m == 0), stop=(m == KC - 1)),
                  [("VB", m % 2), "ONES"], [("ps", 4)])
                A("pe", lambda e: e.matmul(PS[5][:, :Wc], ONES[:], VQ[m % 2][:, :Wc], start=(m == 0), stop=(m == KC - 1)),
                  [("VQ", m % 2), "ONES"], [("ps", 5)])

            wstat_gemm(lambda k0, ksz, mp: w_in_v[:, k0:k0 + ksz, D + mp * 256: D + (mp + 1) * 256], KC, 16,
                       lambda k: XNv[:, k, :Wc], lambda k: [("XN", k)], Wc, evac_v)
            A("dve", lambda e: e.tensor_scalar(MEAN[:, :Wc], PS[4][:, :Wc], 1.0 / D, 0.0, ALU.mult, ALU.add), [("ps", 4)], ["MEAN"])
            A("dve", lambda e: e.tensor_tensor(NMR[:, :Wc], MEAN[:, :Wc], MEAN[:, :Wc], ALU.mult), ["MEAN"], ["NMR"])
            A("dve", lambda e: e.scalar_tensor_tensor(LRS[:, :Wc], PS[5][:, :Wc], 1.0 / D, NMR[:, :Wc], ALU.mult, ALU.subtract),
              [("ps", 5), "NMR"], ["LRS"])
            A("dve", lambda e: e.tensor_scalar(LRS[:, :Wc], LRS[:, :Wc], 1.0, EPS, ALU.mult, ALU.add), ["LRS"], ["LRS"])
            A("act", lambda e: e.activation(LRS[:, :Wc], LRS[:, :Wc], AF.Sqrt), ["LRS"], ["LRS"])
            A("dve", lambda e: e.reciprocal(LRS[:, :Wc], LRS[:, :Wc]), ["LRS"], ["LRS"])
            for m in range(KC):
                A("dve", lambda e, m=m: e.tensor_tensor(VT[:, m, :Wc], VT[:, m, :Wc], MEAN[:, :Wc], ALU.subtract),
                  [("VT", m), "MEAN"], [("VT", m)])
                A("dve", lambda e, m=m: e.tensor_tensor(VT[:, m, :Wc], VT[:, m, :Wc], LRS[:, :Wc], ALU.mult),
                  [("VT", m), "LRS"], [("VT", m)])
                A("act", lambda e, m=m: e.activation(VT[:, m, :Wc], VT[:, m, :Wc], AF.Identity,
                                                     bias=LNGB[:, 1, m:m + 1], scale=LNGB[:, 0, m:m + 1]),
                  [("VT", m), "LNGB"], [("VT", m)])
            if last:
                sp_store(gvT.rearrange("(k p) t -> p k t", p=128), VT[:, :, T - 128:T], [("VT", m) for m in range(KC)])
            if ns:
                sp_store(gvsT.rearrange("(k p) t -> p k t", p=128), VT[:, :, T:T + S], [("VT", m) for m in range(KC)])
                A("dve", lambda e: e.tensor_copy(VNS[:], VT[:, :, T:T + S]), [("VT", m) for m in range(KC)], ["VNS"])
            tb = 0
            for cb in range(NB):
                for mg in range(KC // 4):
                    bank = 6 + (tb % 2)
                    tb += 1
                    for j in range(4):
                        m = mg * 4 + j
                        A("pe", lambda e, bank=bank, j=j, m=m, cb=cb: e.transpose(
                            PS[bank][:, j * 128:(j + 1) * 128], VT[:, m, cb * 128:(cb + 1) * 128], IDF[:]),
                          [("VT", m), "IDF"], [("ps", bank)])
                    A("act" if tb % 2 else "dve", (lambda e, bank=bank, mg=mg, cb=cb: e.activation(
                        VNT[:, cb, mg * 512:(mg + 1) * 512], PS[bank][:], AF.Copy)) if tb % 2 else
                      (lambda e, bank=bank, mg=mg, cb=cb: e.tensor_copy(VNT[:, cb, mg * 512:(mg + 1) * 512], PS[bank][:])),
                      [("ps", bank)], [("VNT", cb, mg)])
            barrier()
            UST = carve(R2, KC * W0 * 2, BF16).rearrange("p (k w) -> p k w", k=KC)
            o = R2 + KC * W0 * 2
            WSF = carve(o, 16 * 128 * 4, F32).rearrange("p (g t) -> p g t", g=16); o += 16 * 128 * 4
            WSB = carve(o, 16 * 128 * 2, BF16).rearrange("p (g t) -> p g t", g=16); o += 16 * 128 * 2
            CAU = carve(o, 128 * 4, F32); o += 128 * 4
            UTMP = [carve(o + i * 2048, 2048, F32) for i in range(2)]; o += 4096
            US16 = [carve(o + i * 1024, 1024, BF16) for i in range(6)]; o += 6144
            SS = carve(o, 4 * S * 4, F32).rearrange("p (j s) -> p j s", j=4); o += 4 * S * 4
            SU = carve(o, 4 * S * 4, F32).rearrange("p (j s) -> p j s", j=4); o += 4 * S * 4
            sp_load(WSF[:].rearrange("p g t -> p (g t)"), wsT, ["WSF"])
            sp_load(CAU, causT, ["CAU"])
            A("dve", lambda e: e.tensor_tensor(WSB[:], WSF[:], CAU.unsqueeze(1).to_broadcast([128, 16, 128]), ALU.mult),
              ["WSF", "CAU"], ["WSB"])
            w_in_u = w_in.rearrange("(k p) c -> p k c", p=128)
            pend = None
            for ct in range(8):
                for kq in range(4):
                    view, sres = load_slot(w_in_u[:, kq * 8:(kq + 1) * 8, ct * 512:(ct + 1) * 512], 8, 512)
                    for kk in range(8):
                        k = kq * 8 + kk
                        for cb in range(NB):
                            A("pe", lambda e, cb=cb, k=k, v=view[:, kk, :]: e.matmul(
                                PS[cb][:], XNv[:, k, cb * 128:(cb + 1) * 128], v, start=(k == 0), stop=(k == KC - 1)),
                              [sres, ("XN", k)], [("ps", cb)])
                        if ns:
                            for j in range(4):
                                A("pe", lambda e, j=j, k=k, v=view[:, kk, j * 128:(j + 1) * 128]: e.matmul(
                                    PS[3][:, j * S:(j + 1) * S], v, XNv[:, k, T:T + S], start=(k == 0 and j == 0), stop=(k == KC - 1),
                                    skip_group_check=True),
                                  [sres, ("XN", k)], [("ps", 3)])
                if pend is not None:
                    pend()
                    pend = None
                cur = []
                for cb in range(NB):
                    sbk = 4 + (cb % 2)
                    for gg in range(2):
                        g = 2 * ct + gg
                        A("pe", lambda e, sbk=sbk, gg=gg, g=g, cb=cb: e.matmul(
                            PS[sbk][:, gg * 256:(gg + 1) * 256], WSB[:, g, :], VNT[:, cb, g * 256:(g + 1) * 256],
                            start=True, stop=True),
                          ["WSB", ("VNT", cb, g // 2)], [("ps", sbk)])
                    ut = UTMP[cb % 2]
                    A("act", lambda e, ut=ut, cb=cb: e.activation(ut, PS[cb][:], AF.Gelu), [("ps", cb)], [("UTMP", cb % 2)])
                    us = US16[(ct % 2) * 3 + cb]
                    for gg in range(2):
                        g = 2 * ct + gg
                        A("dve", lambda e, us=us, ut=ut, sbk=sbk, gg=gg, g=g: e.scalar_tensor_tensor(
                            us[:, gg * 256:(gg + 1) * 256], PS[sbk][:, gg * 256:(gg + 1) * 256], BST[:, g:g + 1],
                            ut[:, gg * 256:(gg + 1) * 256], ALU.add, ALU.mult),
                          [("ps", sbk), ("UTMP", cb % 2), "BST"], [("US16", (ct % 2) * 3 + cb)])
                    cur.append(cb)
                if ns:
                    A("act", lambda e: e.activation(SU[:], PS[3][:, 0:4 * S].rearrange("p (j s) -> p j s", j=4), AF.Gelu),
                      [("ps", 3)], ["SU"])
                    for gg in range(2):
                        g = 2 * ct + gg
                        A("dve", lambda e, gg=gg, g=g, ct=ct: e.tensor_scalar(
                            SS[:, 2 * gg:2 * gg + 2, :], VNS[:, 4 * ct + 2 * gg:4 * ct + 2 * gg + 2, :],
                            WSB0[:, g:g + 1], WSB0[:, 16 + g:17 + g], ALU.mult, ALU.add),
                          ["VNS", "WSB0"], ["SS"])
                    A("dve", lambda e, ct=ct: e.tensor_tensor(UST[:, 4 * ct:4 * ct + 4, T:T + S], SS[:], SU[:], ALU.mult),
                      ["SS", "SU"], [("UST", 4 * ct + j) for j in range(4)])
                def mk_tr(ct=ct):
                    for cb in range(NB):
                        bank = 6 + (cb % 2)
                        pb = PS[bank][:].bitcast(BF16)
                        ui = (ct % 2) * 3 + cb
                        for j in range(4):
                            A("pe", lambda e, pb=pb, j=j, ui=ui: e.transpose(
                                pb[:, j * 128:(j + 1) * 128], US16[ui][:, j * 128:(j + 1) * 128], IDB[:]),
                              [("US16", ui), "IDB"], [("ps", bank)])
                        A("act", lambda e, pb=pb, cb=cb, ct=ct: e.activation(
                            UST[:, 4 * ct:4 * ct + 4, cb * 128:(cb + 1) * 128],
                            pb[:, 0:512].rearrange("p (j t) -> p j t", j=4), AF.Copy),
                          [("ps", bank)], [("UST", 4 * ct + j) for j in range(4)])
                pend = mk_tr
            if pend is not None:
                pend()
                pend = None
            barrier()
            w_out_v = w_out.rearrange("(k p) c -> p k c", p=128)

            def evac_y(m, bank):
                A("act", lambda e: e.activation(Yv[:, m, :Wc], PS[bank][:, :Wc], AF.Copy), [("ps", bank)], [("Y", m)])

            wstat_gemm(lambda k0, ksz, mp: w_out_v[:, k0:k0 + ksz, mp * 256:(mp + 1) * 256], KC, 16,
                       lambda k: UST[:, k, :Wc], lambda k: [("UST", k)], Wc, evac_y, bank0=4)
            post_norm_residual(2, Wc, fuse_next=("ffn0" in stages))
            barrier()

        def ffn(l, p_i, Wc):
            ns = Wc - T
            last = (p_i == npass - 1)
            rmsnorm_to_xn(4 + l, Wc)
            Av = carve(R2, FC * W0 * 2, BF16).rearrange("p (f w) -> p f w", f=FC)
            o = R1
            HB = [carve(o + i * (W0 + 2) * 4, (W0 + 2) * 4, F32) for i in range(2)]; o += 2 * (W0 + 2) * 4
            CB = [carve(o + i * W0 * 4, W0 * 4, F32) for i in range(2)]; o += 2 * W0 * 4
            SG = carve(o, W0 * 4, F32); o += W0 * 4
            ST = carve(o, 2 * FC * S * 2 * 4, F32).rearrange("p (f s r) -> p f s r", f=2 * FC, s=S); o += 2 * FC * S * 2 * 4
            CVS = carve(o, 2 * FC * S * 2 * 4, F32).rearrange("p (f s r) -> p f s r", f=2 * FC, s=S); o += 2 * FC * S * 2 * 4
            assert o <= R2
            sp_load(CW[:].rearrange("p f c -> p (f c)"), convc[l], ["CW"])
            if ns:
                sp_load(ST[:].rearrange("p f s r -> p (f s r)"), stS[l], ["ST"])
            w_up_v = w_up[l].rearrange("(k p) c -> p k c", p=128)
            csn = f"CS{l}"
            for fp in range(FC // 2):
                par = fp % 2
                banks = {}
                for part in range(2):
                    for kg in range(2):
                        c0 = part * DFF + fp * 256
                        view, sres = load_slot(w_up_v[:, kg * 16:(kg + 1) * 16, c0:c0 + 256], 16, 256)
                        for f2 in range(2):
                            bank = par * 4 + part * 2 + f2
                            banks[(part, f2)] = bank
                            for kk in range(16):
                                k = kg * 16 + kk
                                A("pe", lambda e, bank=bank, l_=view[:, kk, f2 * 128:(f2 + 1) * 128], k=k: e.matmul(
                                    PS[bank][:, :Wc], l_, XNv[:, k, :Wc], start=(k == 0), stop=(k == KC - 1)),
                                  [sres, ("XN", k)], [("ps", bank)])
                for f2 in range(2):
                    f = 2 * fp + f2
                    for part in range(2):
                        fi = part * FC + f
                        bank = banks[(part, f2)]
                        hb = HB[part]
                        cbt = CB[part]
                        A("act", lambda e, hb=hb, bank=bank: e.activation(hb[:, 2:2 + Wc], PS[bank][:, :Wc], AF.Copy),
                          [("ps", bank)], [("HB", part)])
                        A("act", lambda e, hb=hb, fi=fi: e.activation(hb[:, 0:2], CS[:, l, fi, :], AF.Copy),
                          [csn], [("HBh", part)])
                        A("dve", lambda e, hb=hb, cbt=cbt, fi=fi: e.tensor_scalar(
                            cbt[:, :Wc], hb[:, 2:2 + Wc], CW[:, fi, 2:3], CW[:, fi, 3:4], ALU.mult, ALU.add),
                          [("HB", part), "CW"], [("CB", part)])
                        A("dve", lambda e, hb=hb, cbt=cbt, fi=fi: e.scalar_tensor_tensor(
                            cbt[:, :T], hb[:, 1:1 + T], CW[:, fi, 1:2], cbt[:, :T], ALU.mult, ALU.add),
                          [("HB", part), ("HBh", part), "CW", ("CB", part)], [("CB", part)])
                        A("dve", lambda e, hb=hb, cbt=cbt, fi=fi: e.scalar_tensor_tensor(
                            cbt[:, :T], hb[:, 0:T], CW[:, fi, 0:1], cbt[:, :T], ALU.mult, ALU.add),
                          [("HB", part), ("HBh", part), "CW", ("CB", part)], [("CB", part)])
                        if ns:
                            A("dve", lambda e, cbt=cbt, fi=fi: e.scalar_tensor_tensor(
                                cbt[:, T:T + S], ST[:, fi, :, 1], CW[:, fi, 1:2], cbt[:, T:T + S], ALU.mult, ALU.add),
                              ["ST", "CW", ("CB", part)], [("CB", part)])
                            A("dve", lambda e, cbt=cbt, fi=fi: e.scalar_tensor_tensor(
                                cbt[:, T:T + S], ST[:, fi, :, 0], CW[:, fi, 0:1], cbt[:, T:T + S], ALU.mult, ALU.add),
                              ["ST", "CW", ("CB", part)], [("CB", part)])
                            A("act", lambda e, hb=hb, fi=fi: e.activation(CVS[:, fi, :, 1], hb[:, 2 + T:2 + T + S], AF.Copy),
                              [("HB", part)], [("CVS", fi)])
                            A("act", lambda e, fi=fi: e.activation(CVS[:, fi, :, 0], ST[:, fi, :, 1], AF.Copy),
                              ["ST"], [("CVS", fi)])
                        A("act", lambda e, hb=hb, fi=fi: e.activation(CS[:, l, fi, :], hb[:, T:T + 2], AF.Copy),
                          [("HB", part), ("HBh", part)], [csn])
                    A("act", lambda e: e.activation(SG[:, :Wc], CB[0][:, :Wc], AF.Silu), [("CB", 0)], ["SG"])
                    A("dve", lambda e, f=f: e.tensor_tensor(Av[:, f, :Wc], SG[:, :Wc], CB[1][:, :Wc], ALU.mult),
                      ["SG", ("CB", 1)], [("A", f)])
            if ns:
                sp_store(cvs_o[l], CVS[:].rearrange("p f s r -> p (f s r)"), [("CVS", fi) for fi in range(2 * FC)])
            if last:
                sp_store(cvp_o[l], CS[:, l].rearrange("p f r -> p (f r)"), [csn])
            barrier()
            w_dn_v = w_down[l].rearrange("(f p) c -> p f c", p=128)

            def evac_y(m, bank):
                A("act", lambda e: e.activation(Yv[:, m, :Wc], PS[bank][:, :Wc], AF.Copy), [("ps", bank)], [("Y", m)])

            wstat_gemm(lambda k0, ksz, mp: w_dn_v[:, k0:k0 + ksz, mp * 256:(mp + 1) * 256], FC, 16,
                       lambda k: Av[:, k, :Wc], lambda k: [("A", k)], Wc, evac_y, bank0=4)
            post_norm_residual(6 + l, Wc, fuse_next=(l == 0 and "attn" in stages))
            barrier()

        c = Ctx()
        c.__dict__.update(locals())
        build_attn(c)

        for p_i in range(npass):
            Wc = W0 if (p_i == 0 and not cfg.get('nosample')) else T
            for g4 in range(4):
                ks = slice(g4 * 8, (g4 + 1) * 8)
                sp_load(XRES[:, ks, 0:T], xT.rearrange("(k p) t -> p k t", p=128)[:, ks, p_i * T:(p_i + 1) * T],
                        [("X", k) for k in range(g4 * 8, (g4 + 1) * 8)])
            if p_i == 0:
                sp_load(XRES[:, :, T:T + S], xsT.rearrange("(k p) t -> p k t", p=128), [("X", k) for k in range(KC)])
            if "gmlp" in stages:
                gmlp(p_i, Wc)
            if "ffn0" in stages:
                ffn(0, p_i, Wc)
            if "attn" in stages:
                c.attn(p_i, Wc)
            if "ffn1" in stages:
                ffn(1, p_i, Wc)
            for g4 in range(4):
                ks = slice(g4 * 8, (g4 + 1) * 8)
                sp_store(yT.rearrange("(k p) t -> p k t", p=128)[:, ks, p_i * T:(p_i + 1) * T], XRES[:, ks, 0:T],
                         [("X", k) for k in range(g4 * 8, (g4 + 1) * 8)])
            if p_i == 0:
                sp_store(ysT.rearrange("(k p) t -> p k t", p=128), XRES[:, :, T:T + S], [("X", k) for k in range(KC)])
        A("sp", lambda e: e.nop(), reads=[], writes=["PHASE"] + [("X", k) for k in range(KC)], ph=False)
        fin = P.ops["sp"][-1]
        for i in range(4):
            stv = P.dma_count.get(f"st{i}", 0) * 16
            if stv and not any(w[0] == "dma" and w[1] == f"st{i}" and w[2] >= stv for w in fin.waits):
                fin.waits.append(("dma", f"st{i}", stv, None))

        with nc.Block() as block:
            P.emit(nc, block, esems, dsems)
    return nc


def build_attn(c):
    A, carve, PS, T_, S_ = c.A, c.carve, c.PS, T, S
    R1, R2, W0 = c.R1, c.R2, c.W0
    XNv, Yv, XRES = c.XNv, c.Yv, c.XRES
    ABI, SINK, IDB, IDF, KPREV, VPREV = c.ABI, c.SINK, c.IDB, c.IDF, c.KPREV, c.VPREV
    load_slot, sp_load, sp_store, barrier = c.load_slot, c.sp_load, c.sp_store, c.barrier
    npass = c.npass
    w_qkv, w_o = c.w_qkv, c.w_o

    def attn(p_i, Wc):
        ns = Wc - T
        last = (p_i == npass - 1)
        c.rmsnorm_to_xn(1, Wc)
        QT = carve(R2, KC * W0 * 2, BF16).rearrange("p (k w) -> p k w", k=KC)
        OT = carve(R2 + KC * W0 * 2, KC * W0 * 2, BF16).rearrange("p (k w) -> p k w", k=KC)
        KW = 128 + W0
        o = R2 + 2 * KC * W0 * 2
        KD = carve(o, 8 * KW * 2, BF16).rearrange("p (h w) -> p h w", h=8); o += 8 * KW * 2
        VDs = carve(o, 64, BF16).rearrange("p (h s) -> p h s", h=8); o += 64
        VROW = carve(o, 256, BF16); o += 256
        BKVt = carve(o, 4096, F32); o += 4096
        KS4 = carve(o, 4096, F32); o += 4096
        SETS = []
        o = 0
        for i_ in range(2):
            d_ = {}
            d_["SM"] = carve(o, 4096, F32).rearrange("p (g k) -> p g k", g=4); o += 4096
            d_["E"] = carve(o, 4096, F32).rearrange("p (g k) -> p g k", g=4); o += 4096
            d_["P16"] = carve(o, 2048, BF16).rearrange("p (g k) -> p g k", g=4); o += 2048
            d_["PT16"] = carve(o, 2048, BF16).rearrange("p (g v q) -> p g v q", g=4, v=2); o += 2048
            d_["STT"] = carve(o, 96, F32).rearrange("p (a g) -> p a g", a=6); o += 96
            SETS.append(d_)
        assert o <= R1
        o = R1
        MASK = carve(o, 2048, F32); o += 2048
        VTOKD = [carve(o + i * 2048, 2048, BF16).rearrange("p (h d) -> p h d", h=8) for i in range(4)]; o += 8192
        KSTG = carve(o, 2048, F32); o += 2048
        KSX = carve(o, 8 * 132 * 2, BF16).rearrange("p (h w) -> p h w", h=8); o += 8 * 132 * 2
        VCD = carve(o, 2048, BF16).rearrange("p (h r d) -> p h r d", h=8, r=2); o += 2048
        VROWS = [carve(o + i * 256, 256, BF16) for i in range(2)]; o += 512
        assert o <= R2
        sp_load(MASK, c.amask, ["MASK"])
        sp_load(BKVt, c.bkv.partition_broadcast(128), ["BKVt"])
        A("dve", lambda e: e.tensor_copy(KD[:, :, 0:128], KPREV[:]), ["KPREV"], ["KDp"])
        wq = w_qkv.rearrange("(k p) c -> p k c", p=128)

        def evac_q(m, bank):
            A("act", lambda e: e.activation(QT[:, m, :Wc], PS[bank][:, :Wc], AF.Identity, bias=ABI[:, m:m + 1]),
              [("ps", bank), "ABI"], [("QT", m)])

        c.wstat_gemm(lambda k0, ksz, mp: wq[:, k0:k0 + ksz, mp * 256:(mp + 1) * 256], KC, 16,
                     lambda k: XNv[:, k, :Wc], lambda k: [("XN", k)], Wc, evac_q, bank0=0)
        wkv = w_qkv.rearrange("(k p) (hh d) -> p k hh d", p=128, d=64)
        for hq in range(4 if ns else 2):
            isv = hq >= 2
            for k4 in range(8):
                i = c.slot_ctr[0] % 3
                c.slot_ctr[0] += 1
                stg = c.SLOTS[i][:, 0:1024].rearrange("p (k c) -> p k c", k=4)
                view = c.SLOTS[i][:, 1024:3072].rearrange("p (k hh r d) -> p k hh r d", k=4, hh=4, r=2)
                lres = ("slot", i)
                sres = ("slotdup", i, 0)
                sres2 = ("slotdup", i, 1)
                A("pool", lambda e, o_=stg, s_=wq[:, k4 * 4:(k4 + 1) * 4, D + hq * 256:D + (hq + 1) * 256]:
                  e.dma_start(out=o_, in_=s_), [], [lres, sres, sres2], dma=f"slot{i}", ph=False)
                stg4 = stg.rearrange("p k (hh d) -> p k hh d", hh=4)
                A("dve", lambda e, o_=view[:, :, :, 0, :], i_=stg4: e.tensor_copy(o_, i_), [lres], [sres])
                A("act", lambda e, o_=view[:, :, :, 1, :], i_=stg4: e.activation(o_, i_, AF.Copy), [lres], [sres2])
                for hh in range(4):
                    for kk in range(4):
                        k = k4 * 4 + kk
                        lw = view[:, kk, hh].rearrange("p r d -> p (r d)")
                        if isv:
                            A("pe", lambda e, hh=hh, lw=lw, k=k: e.matmul(PS[4 + hh][:, :S], lw, XNv[:, k, T:T + S],
                                                                         start=(k == 0), stop=(k == KC - 1)),
                              [sres, sres2, ("XN", k)], [("ps", 4 + hh)])
                        else:
                            A("pe", lambda e, hh=hh, lw=lw, k=k: e.matmul(PS[4 + hh][:, :Wc], lw, XNv[:, k, :Wc],
                                                                         start=(k == 0), stop=(k == KC - 1)),
                              [sres, sres2, ("XN", k)], [("ps", 4 + hh)])
            for hh in range(4):
                h = (hq % 2) * 4 + hh
                if isv:
                    A("act", lambda e, hh=hh, h=h: e.activation(VDs[:, h, :], PS[4 + hh][:, :S], AF.Identity,
                                                              bias=ABI[:, 40 + h:41 + h]),
                      [("ps", 4 + hh), "ABI"], ["VDs"])
                else:
                    A("act", lambda e, hh=hh, h=h: e.activation(KD[:, h, 128:128 + Wc], PS[4 + hh][:, :Wc], AF.Identity,
                                                              bias=ABI[:, 32 + h:33 + h]),
                      [("ps", 4 + hh), "ABI"], [("KD", h)])
        for part in ([1, 0] if (last or ns) else [1]):
            blks = list(range(NB)) if part == 1 else ([NB - 1] if last else [])
            for kq in range(4):
                view, sres = load_slot(wq[:, kq * 8:(kq + 1) * 8, D + part * 512: D + (part + 1) * 512], 8, 512)
                for kk in range(8):
                    k = kq * 8 + kk
                    for cb in blks:
                        A("pe", lambda e, cb=cb, k=k, v=view[:, kk, :]: e.matmul(
                            PS[cb][:], XNv[:, k, cb * 128:(cb + 1) * 128], v, start=(k == 0), stop=(k == KC - 1)),
                          [sres, ("XN", k)], [("ps", cb)])
                    if ns:
                        A("pe", lambda e, k=k, v=view[:, kk, :]: e.matmul(
                            PS[3][:S, :], XNv[:, k, T:T + S], v, start=(k == 0), stop=(k == KC - 1)),
                          [sres, ("XN", k)], [("ps", 3)])
            for cb in blks:
                A("dve", lambda e, cb=cb, part=part: e.tensor_tensor(KSTG, PS[cb][:], BKVt[:, part * 512:(part + 1) * 512], ALU.add),
                  [("ps", cb), "BKVt"], ["KSTG"])
                if part == 1:
                    for r_ in range(2):
                        A("act", lambda e, cb=cb, r_=r_: e.activation(
                            VTOKD[cb + 1].rearrange("p h (r d) -> p h r d", r=2)[:, :, r_, :],
                            KSTG.rearrange("p (h d) -> p h d", h=8), AF.Copy),
                          ["KSTG"], [("VTOKD", cb + 1)])
                if last and cb == NB - 1:
                    sp_store(c.wv_o if part == 1 else c.wk_o, KSTG, ["KSTG"])
            if ns:
                A("dve", lambda e, part=part: e.tensor_tensor(KS4[:S, part * 512:(part + 1) * 512], PS[3][:S, :],
                                                             BKVt[:S, part * 512:(part + 1) * 512], ALU.add),
                  [("ps", 3), "BKVt"], [("KS4", part)])
                dst = c.wvs_o if part == 1 else c.wks_o
                sp_store(dst[:, 127, :], KS4[:S, part * 512:(part + 1) * 512], [("KS4", part)])
                src = c.vc if part == 1 else c.kc_nat
                sp_store(dst[:, 0:127, :], src[:, 1:128, :], [])
        A("dve", lambda e: e.tensor_copy(VTOKD[0].rearrange("p h d -> p (h d)"), VPREV[:]), ["VPREV"], [("VTOKD", 0)])
        barrier()
        if c.cfg.get("attn_cut", 9) < 2:
            return

        ctr = [0]

        def unit(h, gh, c0, nq, kap, nk, mask, vlist, vres, kres=()):
            u = ctr[0]
            ctr[0] += 1
            st_ = SETS[u % 2]
            SM, E, P16, PT16, STT = st_["SM"], st_["E"], st_["P16"], st_["PT16"], st_["STT"]
            MX, NM, RS, DS, DEN, RINV = [STT[:, i, :] for i in range(6)]
            q_ = u % 2

            def R(n):
                return (n, q_)
            sb0 = (u % 2) * 2
            sk = SINK[:nq, h * 8 + gh * 4: h * 8 + gh * 4 + 4]
            for gi in range(4):
                g = gh * 4 + gi
                m, half = 4 * h + g // 2, g % 2
                bank, col = sb0 + half, (gi // 2) * 256
                A("pe", lambda e, bank=bank, col=col, m=m, half=half: e.matmul(
                    PS[bank][:nq, col:col + nk], QT[half * 64:(half + 1) * 64, m, c0:c0 + nq], kap(half), start=True, stop=True),
                  [("QT", m)] + list(kres), [("ps", bank)])
            for b in range(2):
                src = PS[sb0 + b][:nq, :].rearrange("p (a k) -> p a k", a=2)[:, :, :nk]
                if mask is not None:
                    A("dve", lambda e, b=b, src=src: e.tensor_tensor(SM[:nq, b:4:2, :nk], src,
                                                                   mask.unsqueeze(1).to_broadcast([nq, 2, nk]), ALU.add),
                      [("ps", sb0 + b), "MASK"], [R("SM")])
                else:
                    A("dve", lambda e, b=b, src=src: e.tensor_copy(SM[:nq, b:4:2, :nk], src),
                      [("ps", sb0 + b)], [R("SM")])
            A("dve", lambda e: e.tensor_reduce(MX[:nq, :], SM[:nq, :, :nk], AX.X, ALU.max), [R("SM")], [R("MX")])
            A("dve", lambda e: e.scalar_tensor_tensor(MX[:nq, :], MX[:nq, :], SCALE, sk, ALU.mult, ALU.max), [R("MX"), "SINK"], [R("MX")])
            A("dve", lambda e: e.tensor_scalar(NM[:nq, :], MX[:nq, :], -1.0, 0.0, ALU.mult, ALU.add), [R("MX")], [R("NM")])
            A("dve", lambda e: e.tensor_tensor(DS[:nq, :], sk, MX[:nq, :], ALU.subtract), [R("MX"), "SINK"], [R("DS")])
            yield
            for gi in range(4):
                A("act", lambda e, gi=gi: e.activation(E[:nq, gi, :nk], SM[:nq, gi, :nk], AF.Exp,
                                                     bias=NM[:nq, gi:gi + 1], scale=SCALE), [R("SM"), R("NM")], [R("E")])
            A("act", lambda e: e.activation(DS[:nq, :], DS[:nq, :], AF.Exp), [R("DS")], [R("DS")])
            yield
            A("dve", lambda e: e.tensor_reduce(RS[:nq, :], E[:nq, :, :nk], AX.X, ALU.add), [R("E")], [R("RS")])
            A("dve", lambda e: e.tensor_tensor(DEN[:nq, :], RS[:nq, :], DS[:nq, :], ALU.add), [R("RS"), R("DS")], [R("DEN")])
            A("dve", lambda e: e.reciprocal(RINV[:nq, :], DEN[:nq, :]), [R("DEN")], [R("RINV")])
            A("dve", lambda e: e.tensor_tensor(P16[:nq, :, :nk], E[:nq, :, :nk],
                                               RINV[:nq, :].unsqueeze(2).to_broadcast([nq, 4, nk]), ALU.mult),
              [R("E"), R("RINV")], [R("P16")])
            yield
            tbank = 4 + (u % 2)
            pb = PS[tbank][:].bitcast(BF16)
            for gi in range(4):
                for vi, (vl, c_, n) in enumerate(vlist):
                    if n == 1 and nq == 1:
                        continue
                    A("pe", lambda e, gi=gi, vi=vi, c_=c_, n=n: e.transpose(
                        pb[:n, (gi * 2 + vi) * 128:(gi * 2 + vi) * 128 + nq], P16[:nq, gi, c_:c_ + n], IDB[:nq, :nq]),
                      [R("P16"), "IDB"], [("ps", tbank)])
            if nq == 128:
                A("act", lambda e: e.activation(PT16[:], pb.rearrange("p (g v q) -> p g v q", g=4, v=2), AF.Copy),
                  [("ps", tbank)], [R("PT16")])
            else:
                A("act", lambda e: e.activation(PT16[:, :, 0, 0:nq],
                                                pb.rearrange("p (g v q) -> p g v q", g=4, v=2)[:, :, 0, 0:nq], AF.Copy),
                  [("ps", tbank)], [R("PT16")])
            yield
            obank = 6 + (u % 2)
            for gi in range(4):
                for vi, (vl, c_, n) in enumerate(vlist):
                    rhs = P16[0:1, gi, c_:c_ + 1] if (n == 1 and nq == 1) else PT16[:n, gi, vi, :nq]
                    A("pe", lambda e, gi=gi, vi=vi, vl=vl, rhs=rhs: e.matmul(
                        PS[obank][:, gi * 128:gi * 128 + nq], vl, rhs, start=(vi == 0), stop=(vi == len(vlist) - 1)),
                      [R("PT16"), R("P16")] + list(vres), [("ps", obank)])
            for gi in range(4):
                g = gh * 4 + gi
                m, half = 4 * h + g // 2, g % 2
                A("act", lambda e, gi=gi, m=m, half=half: e.activation(
                    OT[half * 64:(half + 1) * 64, m, c0:c0 + nq], PS[obank][half * 64:(half + 1) * 64, gi * 128:gi * 128 + nq], AF.Copy),
                  [("ps", obank)], [("OT", m)])

        def run_units(gens):
            active = []
            it = iter(gens)
            done = False
            while True:
                while len(active) < 2 and not done:
                    try:
                        active.append(next(it))
                    except StopIteration:
                        done = True
                if not active:
                    break
                for g_ in list(active):
                    try:
                        next(g_)
                    except StopIteration:
                        active.remove(g_)

        def prompt_units():
            for cb in range(NB):
                mk = MASK[:, 256:512] if (p_i == 0 and cb == 0) else MASK[:, 0:256]
                for h in range(8):
                    vl = [(VTOKD[cb][:, h, :], 0, 128), (VTOKD[cb + 1][:, h, :], 128, 128)]
                    for gh in range(2):
                        yield unit(h, gh, cb * 128, 128,
                                   lambda half, h=h, cb=cb: KD[half * 64:(half + 1) * 64, h, cb * 128:cb * 128 + 256],
                                   256, mk, vl, [])

        def all_sample_units():
            for s in range(S):
                ksrc = c.kcT[s].rearrange("(h d) t -> d h t", d=64)
                for r_ in range(2):
                    A("pool", lambda e, r_=r_, ksrc=ksrc: e.dma_start(out=KSX[r_ * 64:(r_ + 1) * 64, :, 0:128], in_=ksrc),
                      [], ["KEYS"], dma="smpk", skip_same_dma=True)
                    A("pool", lambda e, r_=r_, s=s: e.dma_start(out=VCD[:, :, r_, :],
                                                               in_=c.vc[s].rearrange("t (h d) -> t h d", h=8)),
                      [], ["VALS"], dma="smpv", skip_same_dma=True)
                A("dve", lambda e, s=s: e.tensor_copy(KSX[:, :, 128:129], KD[:, :, 128 + T + s:128 + T + s + 1]),
                  [("KD", h) for h in range(8)], ["KEYS"])

                def sample_units(s=s):
                    for h in range(8):
                        vr = VROWS[h % 2]
                        pbv = PS[5][:].bitcast(BF16) if False else None
                        yield_prep(h, s, vr)
                        vl = [(VCD[:, h].rearrange("p r d -> p (r d)"), 0, 128), (vr[0:1, :], 128, 1)]
                        for gh in range(2):
                            yield unit(h, gh, T + s, 1, lambda half, h=h: KSX[half * 64:(half + 1) * 64, h, 0:129],
                                       129, None, vl, ["VALS", ("VROW", h % 2)], ["KEYS"])

                def yield_prep(h, s, vr):
                    pbv = PS[3][:].bitcast(BF16)
                    A("pe", lambda e, h=h, s=s, pbv=pbv: e.transpose(pbv[0:1, 0:128], VDs[:, h, s:s + 1], IDB[:]),
                      ["VDs", "IDB"], [("ps", 3)])
                    A("act", lambda e, pbv=pbv, vr=vr: e.activation(vr[0:1, :], pbv[0:1, 0:128], AF.Copy),
                      [("ps", 3)], [("VROW", h % 2)])

                for un in sample_units():
                    yield un

        def interleaved():
            a, b = prompt_units(), (all_sample_units() if ns else iter(()))
            while a is not None or b is not None:
                if a is not None:
                    try:
                        yield next(a)
                    except StopIteration:
                        a = None
                if b is not None:
                    try:
                        yield next(b)
                    except StopIteration:
                        b = None

        run_units(interleaved())
        A("dve", lambda e: e.tensor_copy(KPREV[:], KD[:, :, T:T + 128]), [("KD", h) for h in range(8)], ["KPREV"])
        A("dve", lambda e: e.tensor_copy(VPREV[:], VTOKD[NB].rearrange("p h d -> p (h d)")), [("VTOKD", NB)], ["VPREV"])
        barrier()
        wo = w_o.rearrange("(k p) c -> p k c", p=128)

        def evac_y(m, bank):
            A("act", lambda e: e.activation(Yv[:, m, :Wc], PS[bank][:, :Wc], AF.Identity, bias=ABI[:, 48 + m:49 + m]),
              [("ps", bank), "ABI"], [("Y", m)])

        c.wstat_gemm(lambda k0, ksz, mp: wo[:, k0:k0 + ksz, mp * 256:(mp + 1) * 256], KC, 16,
                     lambda k: OT[:, k, :Wc], lambda k: [("OT", k)], Wc, evac_y, bank0=4)
        c.post_norm_residual(3, Wc, fuse_next=("ffn1" in c.stages))
        barrier()

    c.attn = attn


def _f32(a):
    return np.ascontiguousarray(a, dtype=np.float32)


def prep_core_inputs(inp, core, shared=None):
    s, h = core // 2, core % 2
    start = 0 if h == 0 else 2048 - TOK
    b0 = core * S
    m = {}
    m["xT"] = _f32(inp["x_prompt"][s, start:start + TOK, :].T)
    m["xsT"] = _f32(inp["x_sample"][b0:b0 + S, 0, :].T)

    def fm(v):
        return np.asarray(v).reshape(KC, 128).T

    if shared is None:
        shared = {}
    if not shared:
        g = [inp["norm_mix_pre"][0], inp["norm_mix_pre"][1], inp["norm_mix_post"][0], inp["norm_mix_post"][1],
             inp["norm_ffn_pre"][0], inp["norm_ffn_pre"][1], inp["norm_ffn_post"][0], inp["norm_ffn_post"][1]]
        shared["gains"] = _f32(np.concatenate([fm(v) for v in g], axis=1))
        shared["identf"] = np.eye(128, dtype=np.float32)
        shared["gmlp_w_in"] = _f32(inp["gmlp_w_in"][0])
        shared["gmlp_w_out"] = _f32(inp["gmlp_w_out"][0])
        shared["lngb"] = _f32(np.concatenate([fm(inp["gmlp_ln_g"][0]), fm(inp["gmlp_ln_b"][0])], axis=1))
        ws = np.asarray(inp["gmlp_w_s"][0])
        shared["wsT"] = _f32(ws.transpose(2, 0, 1).reshape(128, 16 * 128))
        shared["causT"] = _f32(np.triu(np.ones((128, 128), np.float32)))
        bs = np.asarray(inp["gmlp_b_s"][0])
        shared["bsT"] = _f32(bs.T)
        shared["wsb0"] = _f32(np.concatenate([ws[:, 0, 0], bs[:, 0]])[None, :])
        shared["ffn_w_up"] = _f32(inp["ffn_w_up"])
        shared["ffn_w_down"] = _f32(inp["ffn_w_down"])
        cw = np.asarray(inp["ffn_conv_w"])
        cb = np.asarray(inp["ffn_conv_b"])
        cc = np.concatenate([cw, cb[:, None, :]], axis=1)
        shared["convc"] = _f32(cc.reshape(2, 4, 2 * FC, 128).transpose(0, 3, 2, 1).reshape(2, 128, 2 * FC * 4))
        shared["attn_w_qkv"] = _f32(inp["attn_w_qkv"][0])
        shared["attn_w_o"] = _f32(inp["attn_w_o"][0])
        bq = np.asarray(inp["attn_b_qkv"][0])
        bkd = np.stack([np.tile(bq[4096 + hh * 64: 4096 + (hh + 1) * 64], 2) for hh in range(16)], axis=1)
        shared["abias"] = _f32(np.concatenate([fm(bq[:4096]), bkd, fm(inp["attn_b_o"][0])], axis=1))
        shared["bkv"] = _f32(bq[4096:][None, :])
        shared["sinks"] = _f32(np.asarray(inp["attn_sinks"][0])[None, :])
        i = np.arange(128)[:, None]
        j = np.arange(256)[None, :]
        ok = (j <= 128 + i) & (j >= i)
        m1 = np.where(ok, 0.0, NEG).astype(np.float32)
        m0 = np.where(ok & (j >= 128), 0.0, NEG).astype(np.float32)
        shared["amask"] = _f32(np.concatenate([m1, m0], axis=1))
    m.update(shared)
    st = np.asarray(inp["state_conv"])[:, b0:b0 + S]
    m["stS"] = _f32(st.reshape(2, S, 2, 2 * FC, 128).transpose(0, 4, 3, 1, 2).reshape(2, 128, 2 * FC * S * 2))
    kc = np.asarray(inp["cache_win_k"])[0, b0:b0 + S].reshape(S, 128, 512)
    vcache = np.asarray(inp["cache_win_v"])[0, b0:b0 + S].reshape(S, 128, 512)
    m["kcT"] = _f32(kc.transpose(0, 2, 1))
    m["vc"] = _f32(vcache)
    m["kc_nat"] = _f32(kc)
    return m


_NC_CACHE = {}


def kernel(**inputs):
    inp = {k: np.asarray(v) for k, v in inputs.items()}
    if "nc" not in _NC_CACHE:
        _NC_CACHE["nc"] = build_nc(dict(npass=NPASS))
    nc = _NC_CACHE["nc"]
    shared = {}
    in_maps = [prep_core_inputs(inp, c, shared) for c in range(8)]
    res = run_bass_kernel_spmd(nc, in_maps, core_ids=list(range(8)))
    r = res.results
    B, L = 4, 2048
    y_prompt = np.empty((B, L, D), np.float32)
    y_sample = np.empty((32, 1, D), np.float32)
    gvp = np.empty((1, B, 128, D), np.float32)
    gvs = np.empty((1, 32, 1, D), np.float32)
    wkp = np.empty((1, B, 128, 8, 64), np.float32)
    wvp = np.empty((1, B, 128, 8, 64), np.float32)
    wks = np.empty((1, 32, 128, 8, 64), np.float32)
    wvs = np.empty((1, 32, 128, 8, 64), np.float32)
    cvp = np.empty((2, B, 2, 2 * DFF), np.float32)
    cvs = np.empty((2, 32, 2, 2 * DFF), np.float32)
    for c in range(8):
        s, h = c // 2, c % 2
        o = r[c]
        if h == 0:
            y_prompt[s, 0:TOK] = o["yT"].T
        else:
            y_prompt[s, TOK:L] = o["yT"][:, 2 * TOK - L:].T
            gvp[0, s] = o["gvT"].T
            wkp[0, s] = o["wk"].reshape(128, 8, 64)
            wvp[0, s] = o["wv"].reshape(128, 8, 64)
            cvp[:, s] = o["cvp"].reshape(2, 128, 2 * FC, 2).transpose(0, 3, 2, 1).reshape(2, 2, 2 * DFF)
        b0 = c * S
        y_sample[b0:b0 + S, 0] = o["ysT"].T
        gvs[0, b0:b0 + S, 0] = o["gvsT"].T
        wks[0, b0:b0 + S] = o["wks"].reshape(S, 128, 8, 64)
        wvs[0, b0:b0 + S] = o["wvs"].reshape(S, 128, 8, 64)
        cvs[:, b0:b0 + S] = o["cvs"].reshape(2, 128, 2 * FC, S, 2).transpose(0, 3, 4, 2, 1).reshape(2, S, 2, 2 * DFF)
    return (y_prompt, y_sample, gvp, gvs, wkp, wvp, wks, wvs, cvp, cvs)
```
